# Optimizing a Trainium2 kernel written in Bass

```python
import math
import jax
import jax.numpy as jnp
from jax import lax
import numpy as np

D_MODEL = 2048
BATCH = 2
SEQ = 4096
DEPTH = 4

SSM_GROUPS = 32
SSM_CH = 16
SSM_STATE = 64
SSM_WIDTH = SSM_GROUPS * SSM_CH
SSM_DT_MIN = 1e-3
SSM_DT_MAX = 1e-1
DN_HEADS = 6
DN_HEAD_DIM = 128
DN_WIDTH = DN_HEADS * DN_HEAD_DIM
DN_CONV = 4
DN_CHUNK = 64
ATTN_HEADS = 6
ATTN_HEAD_DIM = 128
ATTN_WIDTH = ATTN_HEADS * ATTN_HEAD_DIM
DILATED_PAIRS = ((128, 1), (512, 4), (2048, 16))
ATTN_BLOCK = 128
N_BUCKETS = 32
REL_MAX_DIST = 2048
D_MIX = SSM_WIDTH + DN_WIDTH + ATTN_WIDTH
IN_SPLITS = (SSM_WIDTH, ATTN_WIDTH, ATTN_WIDTH, ATTN_WIDTH, 3 * DN_WIDTH, DN_WIDTH, DN_HEADS, DN_HEADS)
N_IN_COLS = SSM_WIDTH + 3 * ATTN_WIDTH + 4 * DN_WIDTH + 2 * DN_HEADS
D_FF = 5632
NORM_EPS = 1e-6
NEG_INF = -1e30

kernel_name = 'hybrid_s5_gdn_dilated_macaron'


def rms_norm(x, gain):
    xf = x.astype(jnp.float32)
    y = xf * lax.rsqrt(jnp.mean(xf * xf, axis=-1, keepdims=True) + NORM_EPS)
    return (y * gain.astype(jnp.float32)).astype(x.dtype)


def l2_normalize(x):
    return x * lax.rsqrt(jnp.sum(x * x, axis=-1, keepdims=True) + NORM_EPS)


def swiglu(h, w_gate, w_up, w_down):
    return (jax.nn.silu(h @ w_gate) * (h @ w_up)) @ w_down


def causal_depthwise_conv(x, w):
    k_width, channels = w.shape
    return lax.conv_general_dilated(x, w[:, None, :], window_strides=(1,), padding=((k_width - 1, 0),),
                                    dimension_numbers=('NWC', 'WIO', 'NWC'), feature_group_count=channels)


def s5_layer(u, lam_re, lam_im, b_re, b_im, c_re, c_im, d_skip, log_dt, glu_w, glu_b):
    bsz, seq, _ = u.shape
    f32 = jnp.float32
    uf = u.astype(f32).reshape(bsz, seq, SSM_GROUPS, SSM_CH)
    lam = lax.complex(lam_re.astype(f32), lam_im.astype(f32))
    dt = jnp.exp(log_dt.astype(f32))[:, None]
    lam_bar = jnp.exp(lam * dt)
    b = lax.complex(b_re.astype(f32), b_im.astype(f32))
    b_bar = ((lam_bar - 1.0) / lam)[..., None] * b
    c = lax.complex(c_re.astype(f32), c_im.astype(f32))
    bu = jnp.einsum('gpc,bsgc->bsgp', b_bar, uf.astype(jnp.complex64))
    a = jnp.broadcast_to(lam_bar, bu.shape)

    def combine(e1, e2):
        a1, b1 = e1
        a2, b2 = e2
        return a1 * a2, a2 * b1 + b2

    _, states = lax.associative_scan(combine, (a, bu), axis=1)
    y = jnp.einsum('gcp,bsgp->bsgc', c, states).real + d_skip.astype(f32).reshape(SSM_GROUPS, SSM_CH) * uf
    y = jax.nn.gelu(y.reshape(bsz, seq, SSM_WIDTH))
    return y * jax.nn.sigmoid(y @ glu_w.astype(f32) + glu_b.astype(f32))


def to_chunks(t, chunk):
    bsz, seq, heads = t.shape[:3]
    t = t.reshape(bsz, seq // chunk, chunk, heads, *t.shape[3:])
    return jnp.moveaxis(t, 3, 1)


def gated_delta_rule(q, k, v, g, beta):
    bsz, seq, heads, dk = q.shape
    dv = v.shape[-1]
    q = to_chunks(q * dk ** -0.5, DN_CHUNK)
    k, v = to_chunks(k, DN_CHUNK), to_chunks(v, DN_CHUNK)
    g, beta = to_chunks(g, DN_CHUNK), to_chunks(beta, DN_CHUNK)
    gc = jnp.cumsum(g, axis=-1)
    idx = jnp.arange(DN_CHUNK)
    causal = idx[:, None] >= idx[None, :]
    strict = idx[:, None] > idx[None, :]
    decay = jnp.exp(jnp.where(causal, gc[..., :, None] - gc[..., None, :], NEG_INF))
    k_beta = k * beta[..., None]
    a_mat = jnp.where(strict, jnp.einsum('bhnck,bhnek->bhnce', k_beta, k) * decay, 0.0)
    eye = jnp.eye(DN_CHUNK, dtype=jnp.float32)
    t_inv = lax.linalg.triangular_solve(eye + a_mat, jnp.broadcast_to(eye, a_mat.shape),
                                        left_side=True, lower=True, unit_diagonal=True)
    u = jnp.einsum('bhnce,bhnev->bhncv', t_inv, v * beta[..., None])
    w = jnp.einsum('bhnce,bhnek->bhnck', t_inv, k_beta * jnp.exp(gc)[..., None])
    attn = jnp.einsum('bhnck,bhnek->bhnce', q, k) * decay
    q_dec = q * jnp.exp(gc)[..., None]
    k_tail = k * jnp.exp(gc[..., -1:] - gc)[..., None]
    chunk_decay = jnp.exp(gc[..., -1])
    xs = tuple(jnp.moveaxis(t, 2, 0) for t in (q_dec, k_tail, u, w, attn, chunk_decay))

    def step(state, inp):
        qd, kt, un, wn, an, dec = inp
        v_new = un - jnp.einsum('bhck,bhkv->bhcv', wn, state)
        o = jnp.einsum('bhck,bhkv->bhcv', qd, state) + jnp.einsum('bhce,bhev->bhcv', an, v_new)
        state = state * dec[..., None, None] + jnp.einsum('bhck,bhcv->bhkv', kt, v_new)
        return state, o

    state0 = jnp.zeros((bsz, heads, dk, dv), jnp.float32)
    _, o = lax.scan(step, state0, xs)
    o = jnp.moveaxis(o, 0, 2).reshape(bsz, heads, seq, dv)
    return jnp.swapaxes(o, 1, 2)


def t5_bucket(dist):
    max_exact = N_BUCKETS // 2
    d = jnp.maximum(dist, 1).astype(jnp.float32)
    large = max_exact + jnp.log(d / max_exact) / math.log(REL_MAX_DIST / max_exact) * (N_BUCKETS - max_exact)
    large = jnp.minimum(large.astype(jnp.int32), N_BUCKETS - 1)
    return jnp.where(dist < max_exact, dist, large)


def dilated_branch(q, k, v, rel_bias, window, dil):
    bsz, seq, heads, dh = q.shape
    blk = ATTN_BLOCK
    sub_len = seq // dil
    n_blocks = -(-sub_len // blk)
    sub_pad = n_blocks * blk

    def to_sub(t):
        t = t.astype(jnp.float32).reshape(bsz, sub_len, dil, heads, dh).transpose(0, 2, 1, 3, 4)
        return jnp.pad(t, ((0, 0), (0, 0), (0, sub_pad - sub_len), (0, 0), (0, 0)))

    def band(t):
        tp = jnp.pad(t, ((0, 0), (0, 0), (blk, 0), (0, 0), (0, 0)))
        prev = tp[:, :, :sub_pad].reshape(bsz, dil, n_blocks, blk, heads, dh)
        cur = tp[:, :, blk:].reshape(bsz, dil, n_blocks, blk, heads, dh)
        return jnp.concatenate([prev, cur], axis=3)

    qb = to_sub(q).reshape(bsz, dil, n_blocks, blk, heads, dh) * dh ** -0.5
    kb = band(to_sub(k))
    vb = band(to_sub(v))
    rel = blk + jnp.arange(blk)[:, None] - jnp.arange(2 * blk)[None, :]
    key_idx = jnp.arange(n_blocks)[:, None] * blk + jnp.arange(2 * blk)[None, :] - blk
    valid = ((rel >= 0) & (rel <= window // dil))[None] & (key_idx >= 0)[:, None, :]
    bias = jnp.moveaxis(rel_bias.astype(jnp.float32)[t5_bucket(jnp.maximum(rel, 0) * dil)], -1, 0)
    logits = jnp.einsum('bgnqhd,bgnkhd->bhgnqk', qb, kb) + bias[None, :, None, None]
    logits = jnp.where(valid[None, None, None], logits, NEG_INF)
    m = jnp.max(logits, axis=-1, keepdims=True)
    p = jnp.exp(logits - m)
    s = jnp.sum(p, axis=-1, keepdims=True)
    o = jnp.einsum('bhgnqk,bgnkhd->bgnqhd', p / s, vb)
    lse = (m + jnp.log(s))[..., 0].reshape(bsz, heads, dil, sub_pad)[..., :sub_len]
    o = o.reshape(bsz, dil, sub_pad, heads, dh)[:, :, :sub_len].transpose(0, 2, 1, 3, 4).reshape(bsz, seq, heads, dh)
    lse = lse.transpose(0, 3, 2, 1).reshape(bsz, seq, heads)
    return o, lse


def dilated_attention(q, k, v, rel_bias):
    outs, lses = [], []
    for window, dil in DILATED_PAIRS:
        o, lse = dilated_branch(q, k, v, rel_bias, window, dil)
        outs.append(o)
        lses.append(lse)
    weights = jax.nn.softmax(jnp.stack(lses, axis=0), axis=0)
    return jnp.einsum('gbsh,gbshd->bshd', weights, jnp.stack(outs, axis=0))


def hybrid_mixer(h, w_in, w_out, ssm_lambda_re, ssm_lambda_im, ssm_b_re, ssm_b_im, ssm_c_re, ssm_c_im,
                 ssm_d, ssm_log_dt, ssm_glu_w, ssm_glu_b, ssm_out_gain, dn_conv_w, dn_a_log, dn_dt_bias,
                 dn_norm_gain, attn_out_gain, rel_bias):
    bsz, seq, _ = h.shape
    f32 = jnp.float32
    proj = h @ w_in
    cuts = [sum(IN_SPLITS[:i + 1]) for i in range(len(IN_SPLITS) - 1)]
    u_ssm, a_q, a_k, a_v, dn_qkv, dn_z, dn_a, dn_b = jnp.split(proj, cuts, axis=-1)

    y_ssm = rms_norm(s5_layer(u_ssm, ssm_lambda_re, ssm_lambda_im, ssm_b_re, ssm_b_im, ssm_c_re, ssm_c_im,
                              ssm_d, ssm_log_dt, ssm_glu_w, ssm_glu_b), ssm_out_gain)

    qkv = jax.nn.silu(causal_depthwise_conv(dn_qkv, dn_conv_w)).astype(f32)
    d_q, d_k, d_v = [t.reshape(bsz, seq, DN_HEADS, DN_HEAD_DIM) for t in jnp.split(qkv, 3, axis=-1)]
    d_q, d_k = l2_normalize(d_q), l2_normalize(d_k)
    beta = jax.nn.sigmoid(dn_b.astype(f32))
    g = -jnp.exp(dn_a_log.astype(f32)) * jax.nn.softplus(dn_a.astype(f32) + dn_dt_bias.astype(f32))
    o_dn = gated_delta_rule(d_q, d_k, d_v, g, beta)
    o_dn = rms_norm(o_dn, dn_norm_gain) * jax.nn.silu(dn_z.astype(f32).reshape(bsz, seq, DN_HEADS, DN_HEAD_DIM))
    y_dn = o_dn.reshape(bsz, seq, DN_WIDTH)

    at_q, at_k, at_v = [t.reshape(bsz, seq, ATTN_HEADS, ATTN_HEAD_DIM) for t in (a_q, a_k, a_v)]
    o_at = dilated_attention(at_q, at_k, at_v, rel_bias).reshape(bsz, seq, ATTN_WIDTH)
    y_at = rms_norm(o_at, attn_out_gain)

    mix = jnp.concatenate([y_ssm.astype(h.dtype), y_dn.astype(h.dtype), y_at.astype(h.dtype)], axis=-1)
    return mix @ w_out


def setup_inputs(seed: int = 0) -> dict:
    key = jax.random.key(seed)
    ks = jax.random.split(key, 24)
    f32 = jnp.float32

    def nrm(k, shape, scale):
        return scale * jax.random.normal(k, shape, f32)

    x = jax.random.normal(ks[0], (BATCH, SEQ, D_MODEL), f32)
    norm_gains = 1.0 + nrm(ks[1], (DEPTH, 6, D_MODEL), 0.05)
    ffn_w_gate = nrm(ks[2], (DEPTH, 2, D_MODEL, D_FF), D_MODEL ** -0.5)
    ffn_w_up = nrm(ks[3], (DEPTH, 2, D_MODEL, D_FF), D_MODEL ** -0.5)
    ffn_w_down = nrm(ks[4], (DEPTH, 2, D_FF, D_MODEL), D_FF ** -0.5)
    w_in = nrm(ks[5], (DEPTH, D_MODEL, N_IN_COLS), D_MODEL ** -0.5)
    w_out = nrm(ks[6], (DEPTH, D_MIX, D_MODEL), D_MIX ** -0.5)
    n_idx = jnp.arange(SSM_STATE, dtype=f32)
    ssm_lambda_re = -0.5 + nrm(ks[7], (DEPTH, SSM_GROUPS, SSM_STATE), 0.01)
    ssm_lambda_im = math.pi * n_idx + nrm(ks[8], (DEPTH, SSM_GROUPS, SSM_STATE), 0.01)
    ssm_b_re = nrm(ks[9], (DEPTH, SSM_GROUPS, SSM_STATE, SSM_CH), (2 * SSM_CH) ** -0.5)
    ssm_b_im = nrm(ks[10], (DEPTH, SSM_GROUPS, SSM_STATE, SSM_CH), (2 * SSM_CH) ** -0.5)
    ssm_c_re = nrm(ks[11], (DEPTH, SSM_GROUPS, SSM_CH, SSM_STATE), (2 * SSM_STATE) ** -0.5)
    ssm_c_im = nrm(ks[12], (DEPTH, SSM_GROUPS, SSM_CH, SSM_STATE), (2 * SSM_STATE) ** -0.5)
    ssm_d = nrm(ks[13], (DEPTH, SSM_WIDTH), 1.0)
    ssm_log_dt = jax.random.uniform(ks[14], (DEPTH, SSM_GROUPS), f32, math.log(SSM_DT_MIN), math.log(SSM_DT_MAX))
    ssm_glu_w = nrm(ks[15], (DEPTH, SSM_WIDTH, SSM_WIDTH), SSM_WIDTH ** -0.5)
    ssm_glu_b = nrm(ks[16], (DEPTH, SSM_WIDTH), 0.02)
    ssm_out_gain = 1.0 + nrm(ks[17], (DEPTH, SSM_WIDTH), 0.05)
    dn_conv_w = nrm(ks[18], (DEPTH, DN_CONV, 3 * DN_WIDTH), DN_CONV ** -0.5)
    dn_a_log = jnp.log(jax.random.uniform(ks[19], (DEPTH, DN_HEADS), f32, 1.0, 16.0))
    dt = jnp.exp(jax.random.uniform(ks[20], (DEPTH, DN_HEADS), f32, math.log(1e-3), math.log(1e-1)))
    dn_dt_bias = dt + jnp.log(-jnp.expm1(-dt))
    dn_norm_gain = 1.0 + nrm(ks[21], (DEPTH, DN_HEAD_DIM), 0.05)
    attn_out_gain = 1.0 + nrm(ks[22], (DEPTH, ATTN_WIDTH), 0.05)
    rel_bias = nrm(ks[23], (N_BUCKETS, ATTN_HEADS), 0.5)
    return {'x': x, 'norm_gains': norm_gains, 'ffn_w_gate': ffn_w_gate, 'ffn_w_up': ffn_w_up,
            'ffn_w_down': ffn_w_down, 'w_in': w_in, 'w_out': w_out,
            'ssm_lambda_re': ssm_lambda_re, 'ssm_lambda_im': ssm_lambda_im,
            'ssm_b_re': ssm_b_re, 'ssm_b_im': ssm_b_im, 'ssm_c_re': ssm_c_re, 'ssm_c_im': ssm_c_im,
            'ssm_d': ssm_d, 'ssm_log_dt': ssm_log_dt, 'ssm_glu_w': ssm_glu_w, 'ssm_glu_b': ssm_glu_b,
            'ssm_out_gain': ssm_out_gain, 'dn_conv_w': dn_conv_w, 'dn_a_log': dn_a_log,
            'dn_dt_bias': dn_dt_bias, 'dn_norm_gain': dn_norm_gain, 'attn_out_gain': attn_out_gain,
            'rel_bias': rel_bias}


def reference(x, norm_gains, ffn_w_gate, ffn_w_up, ffn_w_down, w_in, w_out,
              ssm_lambda_re, ssm_lambda_im, ssm_b_re, ssm_b_im, ssm_c_re, ssm_c_im,
              ssm_d, ssm_log_dt, ssm_glu_w, ssm_glu_b, ssm_out_gain, dn_conv_w, dn_a_log,
              dn_dt_bias, dn_norm_gain, attn_out_gain, rel_bias):
    for l in range(DEPTH):
        gains = norm_gains[l]
        h = rms_norm(x, gains[0])
        x = x + 0.5 * rms_norm(swiglu(h, ffn_w_gate[l, 0], ffn_w_up[l, 0], ffn_w_down[l, 0]), gains[1])
        h = rms_norm(x, gains[2])
        mix = hybrid_mixer(h, w_in[l], w_out[l], ssm_lambda_re[l], ssm_lambda_im[l], ssm_b_re[l], ssm_b_im[l],
                           ssm_c_re[l], ssm_c_im[l], ssm_d[l], ssm_log_dt[l], ssm_glu_w[l], ssm_glu_b[l],
                           ssm_out_gain[l], dn_conv_w[l], dn_a_log[l], dn_dt_bias[l], dn_norm_gain[l],
                           attn_out_gain[l], rel_bias)
        x = x + rms_norm(mix, gains[3])
        h = rms_norm(x, gains[4])
        x = x + 0.5 * rms_norm(swiglu(h, ffn_w_gate[l, 1], ffn_w_up[l, 1], ffn_w_down[l, 1]), gains[5])
    return x
```

```python
import math
import numpy as np
import concourse.bass as bass
import concourse.mybir as mybir
from concourse.bass_utils import run_bass_kernel_spmd

F32 = mybir.dt.float32
BF16 = mybir.dt.bfloat16
AF = mybir.ActivationFunctionType
ALU = mybir.AluOpType
AX = mybir.AxisListType

D_MODEL = 2048
BATCH = 2
SEQ = 4096
DEPTH = 4
D_FF = 5632
N_IN_COLS = 5900
NORM_EPS = 1e-6
NCORES = 8
TOK = BATCH * SEQ // NCORES
NT = TOK // 128
NDC = D_MODEL // 128
NFC = D_FF // 128

ENGS = ("tensor", "vector", "scalar", "gpsimd", "sync")
SEM_LIMIT = 30000
DMA_POOL = 8


class Buf:
    __slots__ = ("name", "w", "r")

    def __init__(self, name=""):
        self.name = name
        self.w = None
        self.r = {}


class _Op:
    __slots__ = ("eng", "fn", "deps", "dma", "idx", "inc", "sem", "val", "guard")

    def __init__(self, eng, fn, deps, dma, idx):
        self.eng, self.fn, self.deps, self.dma, self.idx = eng, fn, deps, dma, idx
        self.inc = dma
        self.sem = None
        self.val = 0
        self.guard = None


class Prog:
    def __init__(self, nc):
        self.nc = nc
        self.ops = []
        self.extra = {e: set() for e in ENGS}
        self.last = {e: None for e in ENGS}
        self.dma_open = set()

    def op(self, eng, fn, reads=(), writes=(), dma=False):
        i = len(self.ops)
        deps = set(self.extra[eng])
        self.extra[eng] = set()
        for b in reads:
            if b.w is not None:
                deps.add(b.w)
        for b in writes:
            if b.w is not None:
                deps.add(b.w)
            deps.update(b.r.values())
        for b in reads:
            b.r[(eng, i) if dma else eng] = i
        for b in writes:
            b.w = i
            b.r = {}
        self.ops.append(_Op(eng, fn, deps, dma, i))
        self.last[eng] = i
        if dma:
            self.dma_open.add(i)
        return i

    def barrier(self):
        lasts = set(v for v in self.last.values() if v is not None) | self.dma_open
        for e in ENGS:
            self.extra[e] |= lasts
        self.dma_open = set()

    def dma(self, out, in_, reads=(), writes=(), eng="sync", **kw):
        return self.op(eng, lambda e: e.dma_start(out=out, in_=in_, **kw), reads, writes, dma=True)

    def emit(self, stack):
        nc = self.nc
        ops = self.ops
        for o in ops:
            for d in o.deps:
                p = ops[d]
                if p.dma:
                    continue
                if p.eng == o.eng and p.eng == "tensor":
                    continue
                p.inc = True
        cnt = {e: 0 for e in ENGS}
        nsem = {e: 0 for e in ENGS}
        for o in ops:
            if o.dma or not o.inc:
                continue
            cnt[o.eng] += 1
            nsem[o.eng] = max(nsem[o.eng], (cnt[o.eng] - 1) // SEM_LIMIT + 1)
        esems = {e: [stack.enter_context(nc.semaphore(f"s_{e}_{k}")) for k in range(nsem[e])] for e in ENGS}
        dma_engs = sorted(set(o.eng for o in ops if o.dma))
        dsems = {e: [stack.enter_context(nc.semaphore(f"d_{e}_{k}")) for k in range(DMA_POOL)] for e in dma_engs}
        cnt = {e: 0 for e in ENGS}
        dcnt = {e: 0 for e in dma_engs}
        duse = {e: [0] * DMA_POOL for e in dma_engs}
        for o in ops:
            if o.dma:
                j = dcnt[o.eng] % DMA_POOL
                dcnt[o.eng] += 1
                prev = duse[o.eng][j]
                o.guard = (dsems[o.eng][j], prev * 16) if prev > 0 else None
                duse[o.eng][j] += 1
                o.sem = dsems[o.eng][j]
                o.val = duse[o.eng][j] * 16
            elif o.inc:
                cnt[o.eng] += 1
                k = (cnt[o.eng] - 1) // SEM_LIMIT
                o.sem = esems[o.eng][k]
                o.val = (cnt[o.eng] - 1) % SEM_LIMIT + 1
        by_eng = {e: [o for o in ops if o.eng == e] for e in ENGS}

        def run(eng_name, e):
            waited = {}

            def wait(sem, val):
                key = id(sem)
                if waited.get(key, 0) < val:
                    e.wait_ge(sem, val)
                    waited[key] = val

            for o in by_eng[eng_name]:
                for d in sorted(o.deps):
                    p = ops[d]
                    if not p.dma and p.eng == eng_name and eng_name == "tensor":
                        continue
                    wait(p.sem, p.val)
                if o.guard is not None:
                    wait(*o.guard)
                ins = o.fn(e)
                if o.inc:
                    ins.then_inc(o.sem, 16 if o.dma else 1)
            if eng_name in dsems:
                for j, s in enumerate(dsems[eng_name]):
                    if duse[eng_name][j] > 0:
                        wait(s, duse[eng_name][j] * 16)

        block = stack.enter_context(nc.Block())

        @block.tensor
        def _(e):
            run("tensor", e)

        @block.vector
        def _(e):
            run("vector", e)

        @block.scalar
        def _(e):
            run("scalar", e)

        @block.gpsimd
        def _(e):
            run("gpsimd", e)

        @block.sync
        def _(e):
            run("sync", e)


class Arena:
    def __init__(self, nc, words):
        self.t = nc.alloc_sbuf_tensor("arena", [128, words], F32)
        self.words = words
        self.off = 0

    def alloc(self, parts, n, dt=F32):
        words = n if dt == F32 else (n + 1) // 2
        assert self.off + words <= self.words, f"SBUF arena overflow {self.off}+{words}>{self.words}"
        ap = self.t[0:parts, self.off:self.off + words]
        if dt != F32:
            ap = ap.bitcast(dt)
        self.off += (words + 7) // 8 * 8
        return ap

    def mark(self):
        return self.off

    def reset(self, m):
        self.off = m


class Ctx:
    def __init__(self, sb_words=50000):
        import contextlib
        self.stack = contextlib.ExitStack()
        self.nc = bass.Bass("TRN2", target_bir_lowering=False)
        self.P = Prog(self.nc)
        self.A = Arena(self.nc, sb_words)
        self.banks = []
        self.bank_bufs = []
        for b in range(8):
            t = self.stack.enter_context(self.nc.psum_tensor(f"ps{b}", [128, 512], F32))
            self.banks.append(t)
            self.bank_bufs.append(Buf(f"ps{b}"))
        self.ident = None

    def dram_in(self, name, shape, dt=F32):
        return self.nc.dram_tensor(name, list(shape), dt, kind="ExternalInput").ap()

    def dram_out(self, name, shape, dt=F32):
        return self.nc.dram_tensor(name, list(shape), dt, kind="ExternalOutput").ap()

    def dram_tmp(self, name, shape, dt=F32):
        return self.nc.dram_tensor(name, list(shape), dt).ap()

    def make_ident(self):
        P, A = self.P, self.A
        idb = A.alloc(128, 128, BF16)
        idf = A.alloc(128, 128, F32)
        b = Buf("ident")
        P.op("gpsimd", lambda e: e.memset(idf, 0.0), writes=[b])
        P.op("gpsimd", lambda e: e.affine_select(out=idf, in_=idf, pattern=[[-1, 128]], compare_op=ALU.not_equal,
                                                 fill=1.0, base=0, channel_multiplier=1), reads=[b], writes=[b])
        P.op("vector", lambda e: e.tensor_copy(out=idb, in_=idf), reads=[b], writes=[b])
        self.identb, self.identf, self.ident_buf = idb, idf, b

    def finish(self):
        self.P.emit(self.stack)
        self.stack.close()
        return self.nc


def rms_rstd(C, src, ss, rstd, junk, n, src_buf, tmp_buf, scale=1.0):
    P = C.P
    P.op("vector", lambda e: e.memset(ss, 0.0), writes=[tmp_buf])
    P.op("scalar", lambda e: e.activation(out=junk, in_=src, func=AF.Square, accum_out=ss),
         reads=[src_buf, tmp_buf], writes=[tmp_buf])
    P.op("vector", lambda e: e.tensor_scalar(out=rstd, in0=ss, scalar1=1.0 / (n * scale * scale),
                                             scalar2=NORM_EPS / (scale * scale), op0=ALU.mult, op1=ALU.add),
         reads=[tmp_buf], writes=[tmp_buf])
    P.op("scalar", lambda e: e.sqrt(out=rstd, in_=rstd), reads=[tmp_buf], writes=[tmp_buf])
    P.op("vector", lambda e: e.reciprocal(out=rstd, in_=rstd), reads=[tmp_buf], writes=[tmp_buf])


def norm_transpose(C, x_dram, gain_b, gain_buf, hT, hT_buf, nt):
    P, A = C.P, C.A
    m = A.mark()
    xt = [A.alloc(128, D_MODEL) for _ in range(2)]
    xb = [Buf("xt0"), Buf("xt1")]
    hb = [A.alloc(128, D_MODEL, BF16) for _ in range(2)]
    hbb = [Buf("hb0"), Buf("hb1")]
    junk = A.alloc(128, D_MODEL, BF16)
    st = [A.alloc(128, 2) for _ in range(2)]
    stb = [Buf("st0"), Buf("st1")]
    for t in range(nt):
        k = t % 2
        P.dma(xt[k], x_dram[t * 128:(t + 1) * 128, :], writes=[xb[k]])
        rms_rstd(C, xt[k], st[k][:, 0:1], st[k][:, 1:2], junk, D_MODEL, xb[k], stb[k])
        P.op("vector", lambda e, k=k: e.scalar_tensor_tensor(out=hb[k], in0=xt[k], scalar=st[k][:, 1:2], in1=gain_b,
                                                              op0=ALU.mult, op1=ALU.mult),
             reads=[xb[k], stb[k], gain_buf], writes=[hbb[k]])
        for half in range(2):
            bank = (2 * t + half) % 8
            pb = C.bank_bufs[bank]
            pt = C.banks[bank][:].bitcast(BF16)
            for c8 in range(8):
                c = half * 8 + c8
                P.op("tensor", lambda e, pt=pt, k=k, c=c, c8=c8: e.transpose(pt[:, c8 * 128:(c8 + 1) * 128],
                                                                             hb[k][:, c * 128:(c + 1) * 128], C.identb),
                     reads=[hbb[k], C.ident_buf], writes=[pb])
            dst = hT[:, half * 8:(half + 1) * 8, t * 128:(t + 1) * 128]
            src = pt.rearrange("p (c k) -> p c k", k=128)
            eng = "scalar" if half == 0 else "vector"
            if eng == "scalar":
                P.op("scalar", lambda e, dst=dst, src=src: e.copy(out=dst, in_=src), reads=[pb], writes=[hT_buf])
            else:
                P.op("vector", lambda e, dst=dst, src=src: e.tensor_copy(out=dst, in_=src), reads=[pb], writes=[hT_buf])


def load_bcast(C, dram_row, n, name):
    t = C.A.alloc(128, n)
    b = Buf(name)
    C.P.dma(t, dram_row.partition_broadcast(128), writes=[b])
    return t, b


def build_ffn(nt=NT, dbg=False):
    C = Ctx(52000)
    P, A = C.P, C.A
    tok = nt * 128
    x = C.dram_in("x", [tok, D_MODEL])
    gains = C.dram_in("gains", [2, D_MODEL])
    wg = C.dram_in("wg", [D_MODEL, D_FF])
    wu = C.dram_in("wu", [D_MODEL, D_FF])
    wd = C.dram_in("wd", [D_FF, D_MODEL])
    xo = C.dram_out("xo", [tok, D_MODEL])
    C.make_ident()
    g_pre, g_pre_b = load_bcast(C, gains[0:1, :], D_MODEL, "g_pre")
    g_post, g_post_b = load_bcast(C, gains[1:2, :], D_MODEL, "g_post")
    aT = A.alloc(128, NFC * tok, BF16).rearrange("p (j t) -> p j t", t=tok)
    aT_bufs = [Buf(f"aT{j}") for j in range(NFC)]
    m0 = A.mark()
    hT = A.alloc(128, NDC * tok, BF16).rearrange("p (c t) -> p c t", t=tok)
    hT_buf = Buf("hT")
    norm_transpose(C, x, g_pre, g_pre_b, hT, hT_buf, nt)
    FW = 256
    wgt = [A.alloc(128, NDC * FW, BF16).rearrange("p (c f) -> p c f", f=FW) for _ in range(2)]
    wut = [A.alloc(128, NDC * FW, BF16).rearrange("p (c f) -> p c f", f=FW) for _ in range(2)]
    wgb = [Buf("wg0"), Buf("wg1")]
    wub = [Buf("wu0"), Buf("wu1")]
    sg = [A.alloc(128, 512) for _ in range(2)]
    sgb = [Buf("sg0"), Buf("sg1")]
    wg_v = wg.rearrange("(c p) f -> p c f", p=128)
    wu_v = wu.rearrange("(c p) f -> p c f", p=128)
    ntg = (tok + 511) // 512
    tgw = min(512, tok)
    it = 0
    for jj in range(D_FF // FW):
        k = jj % 2
        P.dma(wgt[k], wg_v[:, :, jj * FW:(jj + 1) * FW], writes=[wgb[k]], eng="gpsimd")
        P.dma(wut[k], wu_v[:, :, jj * FW:(jj + 1) * FW], writes=[wub[k]], eng="gpsimd")
        for js in range(FW // 128):
            j = jj * (FW // 128) + js
            for tg in range(ntg):
                bg = (2 * it) % 8
                bu = (2 * it + 1) % 8
                it += 1
                pg, pu = C.banks[bg][:, 0:tgw], C.banks[bu][:, 0:tgw]
                for c in range(NDC):
                    P.op("tensor", lambda e, pg=pg, k=k, c=c, js=js, tg=tg: e.matmul(
                        pg, lhsT=wgt[k][:, c, js * 128:(js + 1) * 128], rhs=hT[:, c, tg * 512:tg * 512 + tgw],
                        start=(c == 0), stop=(c == NDC - 1)), reads=[wgb[k], hT_buf], writes=[C.bank_bufs[bg]])
                for c in range(NDC):
                    P.op("tensor", lambda e, pu=pu, k=k, c=c, js=js, tg=tg: e.matmul(
                        pu, lhsT=wut[k][:, c, js * 128:(js + 1) * 128], rhs=hT[:, c, tg * 512:tg * 512 + tgw],
                        start=(c == 0), stop=(c == NDC - 1)), reads=[wub[k], hT_buf], writes=[C.bank_bufs[bu]])
                s = it % 2
                P.op("scalar", lambda e, s=s, pg=pg: e.activation(out=sg[s][:, 0:tgw], in_=pg, func=AF.Silu),
                     reads=[C.bank_bufs[bg]], writes=[sgb[s]])
                P.op("vector", lambda e, s=s, pu=pu, j=j, tg=tg: e.tensor_tensor(
                    out=aT[:, j, tg * 512:tg * 512 + tgw], in0=sg[s][:, 0:tgw], in1=pu, op=ALU.mult),
                    reads=[sgb[s], C.bank_bufs[bu]], writes=[aT_bufs[j]])
    P.barrier()
    if dbg:
        d_hT = C.dram_out("d_hT", [128, NDC * tok], BF16)
        d_aT = C.dram_out("d_aT", [128, NFC * tok], BF16)
        P.dma(d_hT, hT.rearrange("p c t -> p (c t)"), reads=[hT_buf])
        P.dma(d_aT, aT.rearrange("p j t -> p (j t)"), reads=aT_bufs)
        P.barrier()
    A.reset(m0)
    y = [A.alloc(128, D_MODEL) for _ in range(nt)]
    yb = [Buf(f"y{t}") for t in range(nt)]
    KG = 4
    wdt = [A.alloc(128, KG * 512, BF16).rearrange("p (k d) -> p k d", d=512) for _ in range(3)]
    wdb = [Buf("wd0"), Buf("wd1"), Buf("wd2")]
    wd_v = wd.rearrange("(k p) d -> p k d", p=128)
    n = 0
    for q in range(4):
        for kg in range(NFC // KG):
            s = n % 3
            n += 1
            P.dma(wdt[s], wd_v[:, kg * KG:(kg + 1) * KG, q * 512:(q + 1) * 512], writes=[wdb[s]], eng="gpsimd")
            for ks in range(KG):
                kk = kg * KG + ks
                for t in range(nt):
                    P.op("tensor", lambda e, t=t, kk=kk, s=s, ks=ks: e.matmul(
                        C.banks[t][:], lhsT=aT[:, kk, t * 128:(t + 1) * 128], rhs=wdt[s][:, ks, :],
                        start=(kk == 0), stop=(kk == NFC - 1)), reads=[aT_bufs[kk], wdb[s]], writes=[C.bank_bufs[t]])
        for t in range(nt):
            if t % 2 == 0:
                P.op("scalar", lambda e, t=t, q=q: e.copy(out=y[t][:, q * 512:(q + 1) * 512], in_=C.banks[t][:]),
                     reads=[C.bank_bufs[t]], writes=[yb[t]])
            else:
                P.op("vector", lambda e, t=t, q=q: e.tensor_copy(out=y[t][:, q * 512:(q + 1) * 512], in_=C.banks[t][:]),
                     reads=[C.bank_bufs[t]], writes=[yb[t]])
    if dbg:
        d_y = C.dram_out("d_y", [tok, D_MODEL])
        for t in range(nt):
            P.dma(d_y[t * 128:(t + 1) * 128, :], y[t], reads=[yb[t]])
        P.barrier()
    xt = [A.alloc(128, D_MODEL) for _ in range(2)]
    xb = [Buf("ex0"), Buf("ex1")]
    junk = A.alloc(128, D_MODEL, BF16)
    st = [A.alloc(128, 2) for _ in range(2)]
    stb = [Buf("est0"), Buf("est1")]
    for t in range(nt):
        k = t % 2
        P.dma(xt[k], x[t * 128:(t + 1) * 128, :], writes=[xb[k]])
        rms_rstd(C, y[t], st[k][:, 0:1], st[k][:, 1:2], junk, D_MODEL, yb[t], stb[k], scale=0.5)
        P.op("vector", lambda e, t=t, k=k: e.scalar_tensor_tensor(out=y[t], in0=y[t], scalar=st[k][:, 1:2], in1=g_post,
                                                                   op0=ALU.mult, op1=ALU.mult),
             reads=[yb[t], stb[k], g_post_b], writes=[yb[t]])
        P.op("gpsimd", lambda e, t=t, k=k: e.tensor_tensor(out=y[t], in0=y[t], in1=xt[k], op=ALU.add),
             reads=[yb[t], xb[k]], writes=[yb[t]])
        P.dma(xo[t * 128:(t + 1) * 128, :], y[t], reads=[yb[t]])
    return C.finish()


NPC = 48


def build_proj(nt=NT):
    C = Ctx(52000)
    P, A = C.P, C.A
    tok = nt * 128
    x = C.dram_in("x", [tok, D_MODEL])
    gains = C.dram_in("gains", [1, D_MODEL])
    win = C.dram_in("win", [D_MODEL, NPC * 128])
    projT = C.dram_out("projT", [NPC * 128, tok])
    C.make_ident()
    g_pre, g_pre_b = load_bcast(C, gains[0:1, :], D_MODEL, "g_pre")
    hT = A.alloc(128, NDC * tok, BF16).rearrange("p (c t) -> p c t", t=tok)
    hT_buf = Buf("hT")
    norm_transpose(C, x, g_pre, g_pre_b, hT, hT_buf, nt)
    FW = 256
    wt = [A.alloc(128, NDC * FW, BF16).rearrange("p (c f) -> p c f", f=FW) for _ in range(2)]
    wb = [Buf("w0"), Buf("w1")]
    ot = [A.alloc(128, 512) for _ in range(4)]
    ob = [Buf(f"o{i}") for i in range(4)]
    win_v = win.rearrange("(c p) f -> p c f", p=128)
    ntg = (tok + 511) // 512
    tgw = min(512, tok)
    it = 0
    for jj in range(NPC * 128 // FW):
        k = jj % 2
        P.dma(wt[k], win_v[:, :, jj * FW:(jj + 1) * FW], writes=[wb[k]], eng="gpsimd")
        for js in range(FW // 128):
            j = jj * (FW // 128) + js
            for tg in range(ntg):
                bk = it % 8
                s = it % 4
                it += 1
                ps = C.banks[bk][:, 0:tgw]
                for c in range(NDC):
                    P.op("tensor", lambda e, ps=ps, k=k, c=c, js=js, tg=tg: e.matmul(
                        ps, lhsT=wt[k][:, c, js * 128:(js + 1) * 128], rhs=hT[:, c, tg * 512:tg * 512 + tgw],
                        start=(c == 0), stop=(c == NDC - 1)), reads=[wb[k], hT_buf], writes=[C.bank_bufs[bk]])
                if it % 2 == 0:
                    P.op("scalar", lambda e, s=s, ps=ps: e.copy(out=ot[s][:, 0:tgw], in_=ps),
                         reads=[C.bank_bufs[bk]], writes=[ob[s]])
                else:
                    P.op("vector", lambda e, s=s, ps=ps: e.tensor_copy(out=ot[s][:, 0:tgw], in_=ps),
                         reads=[C.bank_bufs[bk]], writes=[ob[s]])
                P.dma(projT[j * 128:(j + 1) * 128, tg * 512:tg * 512 + tgw], ot[s][:, 0:tgw], reads=[ob[s]])
    return C.finish()


TWO_PI = 2.0 * math.pi
MAGIC = 12582912.0


def build_s5(T=SEQ):
    C = Ctx(52000)
    P, A = C.P, C.A
    uT = C.dram_in("uT", [128, T])
    lam = C.dram_in("lam", [128, 12])
    bmat = C.dram_in("bmat", [4, 2, 128, 16])
    cmat = C.dram_in("cmat", [4, 2, 128, 16])
    dvec = C.dram_in("dvec", [32, 4])
    yT = C.dram_out("yT", [128, T])
    C.make_ident()
    NCH = T // 512

    def vop(fn, reads, writes, eng="vector"):
        return P.op(eng, fn, reads, writes)

    cb = Buf("consts")
    Blo = A.alloc(128, T)
    Ahi = A.alloc(128, T)
    ti_i = Ahi.bitcast(mybir.dt.int32)
    P.op("gpsimd", lambda e: e.iota(ti_i, pattern=[[1, T]], base=0, channel_multiplier=0), writes=[cb])
    vop(lambda e: e.tensor_copy(out=Blo, in_=ti_i), [cb], [cb])
    vop(lambda e: e.tensor_scalar(out=Ahi, in0=Blo, scalar1=1.0 / 64, scalar2=-63.0 / 128, op0=ALU.mult, op1=ALU.add), [cb], [cb])
    vop(lambda e: e.tensor_scalar(out=Ahi, in0=Ahi, scalar1=MAGIC, scalar2=None, op0=ALU.add), [cb], [cb])
    vop(lambda e: e.tensor_scalar(out=Ahi, in0=Ahi, scalar1=MAGIC, scalar2=None, op0=ALU.subtract), [cb], [cb])
    vop(lambda e: e.scalar_tensor_tensor(out=Blo, in0=Ahi, scalar=-64.0, in1=Blo, op0=ALU.mult, op1=ALU.add), [cb], [cb])
    pb = Buf("params")
    lt = A.alloc(128, 12)
    P.dma(lt, lam, writes=[pb])
    prm = A.alloc(128, 64)
    col = lambda i: prm[:, 4 * i:4 * i + 4]
    dt_, ar, th2, r_, fr, afr, sn, cs, lbr, lbi, den, kre, kim, f64, tA, tB = [col(i) for i in range(16)]
    lre, lim, ldt = lt[:, 0:4], lt[:, 4:8], lt[:, 8:12]
    hpi = A.alloc(128, 1)
    vop(lambda e: e.memset(hpi, math.pi / 2), [], [pb])
    sop = lambda fn: P.op("scalar", fn, [pb], [pb])
    pv = lambda fn: P.op("vector", fn, [pb], [pb])
    sop(lambda e: e.activation(out=dt_, in_=ldt, func=AF.Exp))
    pv(lambda e: e.tensor_tensor(out=ar, in0=lre, in1=dt_, op=ALU.mult))
    pv(lambda e: e.scalar_tensor_tensor(out=th2, in0=lim, scalar=1.0 / TWO_PI, in1=dt_, op0=ALU.mult, op1=ALU.mult))
    sop(lambda e: e.activation(out=r_, in_=ar, func=AF.Exp))
    pv(lambda e: e.tensor_scalar(out=tA, in0=th2, scalar1=MAGIC, scalar2=None, op0=ALU.add))
    pv(lambda e: e.tensor_scalar(out=tA, in0=tA, scalar1=MAGIC, scalar2=None, op0=ALU.subtract))
    pv(lambda e: e.tensor_tensor(out=fr, in0=th2, in1=tA, op=ALU.subtract))
    pv(lambda e: e.scalar_tensor_tensor(out=afr, in0=fr, scalar=-1.0, in1=fr, op0=ALU.mult, op1=ALU.max))
    sop(lambda e: e.activation(out=sn, in_=fr, func=AF.Sin, scale=TWO_PI))
    sop(lambda e: e.activation(out=cs, in_=afr, func=AF.Sin, scale=-TWO_PI, bias=hpi))
    pv(lambda e: e.tensor_tensor(out=lbr, in0=r_, in1=cs, op=ALU.mult))
    pv(lambda e: e.tensor_tensor(out=lbi, in0=r_, in1=sn, op=ALU.mult))
    pv(lambda e: e.tensor_scalar(out=lbr, in0=lbr, scalar1=-1.0, scalar2=None, op0=ALU.add))
    pv(lambda e: e.tensor_tensor(out=den, in0=lre, in1=lre, op=ALU.mult))
    pv(lambda e: e.tensor_tensor(out=tA, in0=lim, in1=lim, op=ALU.mult))
    pv(lambda e: e.tensor_tensor(out=den, in0=den, in1=tA, op=ALU.add))
    pv(lambda e: e.reciprocal(out=den, in_=den))
    pv(lambda e: e.tensor_tensor(out=tA, in0=lbr, in1=lre, op=ALU.mult))
    pv(lambda e: e.tensor_tensor(out=tB, in0=lbi, in1=lim, op=ALU.mult))
    pv(lambda e: e.tensor_tensor(out=kre, in0=tA, in1=tB, op=ALU.add))
    pv(lambda e: e.tensor_tensor(out=kre, in0=kre, in1=den, op=ALU.mult))
    pv(lambda e: e.tensor_tensor(out=tA, in0=lbi, in1=lre, op=ALU.mult))
    pv(lambda e: e.tensor_tensor(out=tB, in0=lbr, in1=lim, op=ALU.mult))
    pv(lambda e: e.tensor_tensor(out=kim, in0=tA, in1=tB, op=ALU.subtract))
    pv(lambda e: e.tensor_tensor(out=kim, in0=kim, in1=den, op=ALU.mult))
    pv(lambda e: e.tensor_scalar(out=f64, in0=th2, scalar1=64.0, scalar2=None, op0=ALU.mult))
    pv(lambda e: e.tensor_scalar(out=tA, in0=f64, scalar1=MAGIC, scalar2=None, op0=ALU.add))
    pv(lambda e: e.tensor_scalar(out=tA, in0=tA, scalar1=MAGIC, scalar2=None, op0=ALU.subtract))
    pv(lambda e: e.tensor_tensor(out=f64, in0=f64, in1=tA, op=ALU.subtract))

    W = [A.alloc(128, T) for _ in range(6)]
    Wb = [Buf(f"W{i}") for i in range(6)]
    up = A.alloc(32, T)
    upb = Buf("up")
    yp = A.alloc(32, T)
    ypb = Buf("yp")
    BB = [A.alloc(128, 32) for _ in range(2)]
    BBb = Buf("BB")
    CCt = [A.alloc(128, 32) for _ in range(2)]
    CCb = Buf("CC")
    Bst = [A.alloc(128, 32) for _ in range(2)]
    LT = [A.alloc(32, 128) for _ in range(2)]
    LTb = Buf("LT")
    dp = A.alloc(32, 4)
    dpb = Buf("dp")
    P.dma(dp, dvec, writes=[dpb])
    cos_, sin_, bre, bim, t1, t2 = W
    cosb, sinb, breb, bimb, t1b, t2b = Wb
    bank = 0
    for j in range(4):
        kr, ki = kre[:, j:j + 1], kim[:, j:j + 1]
        for ri in range(2):
            P.op("gpsimd", lambda e, ri=ri: e.memset(Bst[ri], 0.0), [], [BBb])
            P.op("gpsimd", lambda e, ri=ri: e.memset(CCt[ri], 0.0), [], [CCb])
        for ri in range(2):
            for g2 in range(2):
                P.dma(Bst[ri][g2 * 64:(g2 + 1) * 64, g2 * 16:(g2 + 1) * 16], bmat[j, ri, g2 * 64:(g2 + 1) * 64, :], writes=[BBb])
                P.dma(CCt[ri][g2 * 64:(g2 + 1) * 64, g2 * 16:(g2 + 1) * 16], cmat[j, ri, g2 * 64:(g2 + 1) * 64, :], writes=[CCb])
        vop(lambda e, ki=ki: e.tensor_scalar(out=BB[0], in0=Bst[1], scalar1=ki, scalar2=-1.0, op0=ALU.mult, op1=ALU.mult), [BBb, pb], [BBb])
        vop(lambda e, kr=kr: e.scalar_tensor_tensor(out=BB[0], in0=Bst[0], scalar=kr, in1=BB[0], op0=ALU.mult, op1=ALU.add), [BBb, pb], [BBb])
        vop(lambda e, ki=ki: e.tensor_scalar(out=BB[1], in0=Bst[0], scalar1=ki, scalar2=None, op0=ALU.mult), [BBb, pb], [BBb])
        vop(lambda e, kr=kr: e.scalar_tensor_tensor(out=BB[1], in0=Bst[1], scalar=kr, in1=BB[1], op0=ALU.mult, op1=ALU.add), [BBb, pb], [BBb])
        vop(lambda e: e.tensor_scalar(out=CCt[1], in0=CCt[1], scalar1=-1.0, scalar2=None, op0=ALU.mult), [CCb], [CCb])
        for ri in range(2):
            bk = bank % 8
            bank += 1
            P.op("tensor", lambda e, ri=ri, bk=bk: e.transpose(C.banks[bk][0:32, 0:128], BB[ri], C.identf),
                 [BBb, C.ident_buf], [C.bank_bufs[bk]])
            P.op("scalar", lambda e, ri=ri, bk=bk: e.copy(out=LT[ri], in_=C.banks[bk][0:32, 0:128]), [C.bank_bufs[bk]], [LTb])
        P.dma(up, uT[32 * j:32 * (j + 1), :], writes=[upb])
        vop(lambda e, j=j: e.tensor_scalar(out=t1, in0=Ahi, scalar1=f64[:, j:j + 1], scalar2=None, op0=ALU.mult), [cb, pb], [t1b])
        vop(lambda e, j=j: e.scalar_tensor_tensor(out=t1, in0=Blo, scalar=th2[:, j:j + 1], in1=t1, op0=ALU.mult, op1=ALU.add), [cb, pb, t1b], [t1b])
        vop(lambda e: e.tensor_scalar(out=t2, in0=t1, scalar1=MAGIC, scalar2=None, op0=ALU.add), [t1b], [t2b])
        vop(lambda e: e.tensor_scalar(out=t2, in0=t2, scalar1=MAGIC, scalar2=None, op0=ALU.subtract), [t2b], [t2b])
        P.op("gpsimd", lambda e: e.tensor_tensor(out=t1, in0=t1, in1=t2, op=ALU.subtract), [t1b, t2b], [t1b])
        P.op("scalar", lambda e: e.activation(out=sin_, in_=t1, func=AF.Sin, scale=TWO_PI), [t1b], [sinb])
        vop(lambda e: e.scalar_tensor_tensor(out=t2, in0=t1, scalar=-1.0, in1=t1, op0=ALU.mult, op1=ALU.max), [t1b], [t2b])
        P.op("scalar", lambda e: e.activation(out=cos_, in_=t2, func=AF.Sin, scale=-TWO_PI, bias=hpi), [t2b, pb], [cosb])
        for q in range(NCH):
            for ri, (dst, dstb) in enumerate(((bre, breb), (bim, bimb))):
                bk = bank % 8
                bank += 1
                P.op("tensor", lambda e, ri=ri, bk=bk, q=q: e.matmul(C.banks[bk][:], lhsT=LT[ri], rhs=up[:, q * 512:(q + 1) * 512],
                                                                    start=True, stop=True), [LTb, upb], [C.bank_bufs[bk]])
                P.op("scalar", lambda e, dst=dst, bk=bk, q=q: e.copy(out=dst[:, q * 512:(q + 1) * 512], in_=C.banks[bk][:]),
                     [C.bank_bufs[bk]], [dstb])
        vop(lambda e: e.tensor_tensor(out=t1, in0=cos_, in1=bre, op=ALU.mult), [cosb, breb], [t1b])
        P.op("gpsimd", lambda e: e.tensor_tensor(out=t2, in0=sin_, in1=bim, op=ALU.mult), [sinb, bimb], [t2b])
        vop(lambda e: e.tensor_tensor(out=t1, in0=t1, in1=t2, op=ALU.add), [t1b, t2b], [t1b])
        P.op("gpsimd", lambda e: e.tensor_tensor(out=t2, in0=cos_, in1=bim, op=ALU.mult), [cosb, bimb], [t2b])
        vop(lambda e: e.tensor_tensor(out=bre, in0=sin_, in1=bre, op=ALU.mult), [sinb, breb], [breb])
        P.op("gpsimd", lambda e: e.tensor_tensor(out=t2, in0=t2, in1=bre, op=ALU.subtract), [t2b, breb], [t2b])
        rj = r_[:, j:j + 1].to_broadcast([128, T])
        vop(lambda e, rj=rj: e.tensor_tensor_scan(out=bre, data0=rj, data1=t1, initial=0.0, op0=ALU.mult, op1=ALU.add), [t1b, pb], [breb])
        vop(lambda e, rj=rj: e.tensor_tensor_scan(out=bim, data0=rj, data1=t2, initial=0.0, op0=ALU.mult, op1=ALU.add), [t2b, pb], [bimb])
        vop(lambda e: e.tensor_tensor(out=t1, in0=cos_, in1=bre, op=ALU.mult), [cosb, breb], [t1b])
        P.op("gpsimd", lambda e: e.tensor_tensor(out=t2, in0=sin_, in1=bim, op=ALU.mult), [sinb, bimb], [t2b])
        vop(lambda e: e.tensor_tensor(out=t1, in0=t1, in1=t2, op=ALU.subtract), [t1b, t2b], [t1b])
        P.op("gpsimd", lambda e: e.tensor_tensor(out=bre, in0=sin_, in1=bre, op=ALU.mult), [sinb, breb], [breb])
        vop(lambda e: e.tensor_tensor(out=bim, in0=cos_, in1=bim, op=ALU.mult), [cosb, bimb], [bimb])
        P.op("gpsimd", lambda e: e.tensor_tensor(out=t2, in0=bre, in1=bim, op=ALU.add), [breb, bimb], [t2b])
        for q in range(NCH):
            bk = bank % 8
            bank += 1
            ps = C.banks[bk][0:32, :]
            P.op("tensor", lambda e, ps=ps, q=q: e.matmul(ps, lhsT=CCt[0], rhs=t1[:, q * 512:(q + 1) * 512], start=True, stop=False),
                 [CCb, t1b], [C.bank_bufs[bk]])
            P.op("tensor", lambda e, ps=ps, q=q: e.matmul(ps, lhsT=CCt[1], rhs=t2[:, q * 512:(q + 1) * 512], start=False, stop=True),
                 [CCb, t2b], [C.bank_bufs[bk]])
            vop(lambda e, ps=ps, q=q, j=j: e.scalar_tensor_tensor(out=yp[:, q * 512:(q + 1) * 512], in0=up[:, q * 512:(q + 1) * 512],
                                                                  scalar=dp[:, j:j + 1], in1=ps, op0=ALU.mult, op1=ALU.add),
                [upb, dpb, C.bank_bufs[bk]], [ypb])
        P.dma(yT[32 * j:32 * (j + 1), :], yp, reads=[ypb])
    return C.finish()


def s5_inputs(uT, lam_re, lam_im, log_dt, b_re, b_im, c_re, c_im, d):
    lam = np.empty((128, 12), np.float32)
    bmat = np.empty((4, 2, 128, 16), np.float32)
    cmat = np.empty((4, 2, 128, 16), np.float32)
    dvec = np.empty((32, 4), np.float32)
    for j in range(4):
        for g2 in range(2):
            g = 2 * j + g2
            sl = slice(g2 * 64, (g2 + 1) * 64)
            lam[sl, j] = lam_re[g]
            lam[sl, 4 + j] = lam_im[g]
            lam[sl, 8 + j] = log_dt[g]
            bmat[j, 0, sl] = b_re[g]
            bmat[j, 1, sl] = b_im[g]
            cmat[j, 0, sl] = c_re[g].T
            cmat[j, 1, sl] = c_im[g].T
            dvec[g2 * 16:(g2 + 1) * 16, j] = d[g * 16:(g + 1) * 16]
    return dict(uT=np.ascontiguousarray(uT, dtype=np.float32), lam=lam, bmat=bmat, cmat=cmat, dvec=dvec)


DILS = (1, 4, 16)
ATT_NEG = -30000.0
N_BUCKETS = 32
REL_MAX_DIST = 2048


def attn_bias_tables(rel_bias):
    H = rel_bias.shape[1]
    k = np.arange(128)[:, None]
    q = np.arange(128)[None, :]
    out = np.empty((H, 3, 128, 2, 128), np.float32)
    max_exact = N_BUCKETS // 2
    for bi, d in enumerate(DILS):
        for pc in range(2):
            rel = (128 + q - k) if pc == 0 else (q - k)
            valid = (rel >= 0) & (rel <= 128)
            dist = np.maximum(rel, 0) * d
            dd = np.maximum(dist, 1).astype(np.float32)
            large = max_exact + np.log(dd / max_exact) / math.log(REL_MAX_DIST / max_exact) * (N_BUCKETS - max_exact)
            large = np.minimum(large.astype(np.int32), N_BUCKETS - 1)
            bucket = np.where(dist < max_exact, dist, large)
            for h in range(H):
                out[h, bi, :, pc, :] = np.where(valid, rel_bias[bucket, h], np.float32(ATT_NEG))
    return out


def build_attn(T=SEQ, NH=2):
    C = Ctx(52000)
    P, A = C.P, C.A
    qkvT = C.dram_in("qkvT", [NH, 3, 128, T])
    bt_d = C.dram_in("bt", [NH, 3, 128, 256])
    oT = C.dram_out("oT", [NH, 128, T])
    C.make_ident()
    ones = A.alloc(128, 128, BF16)
    onesb = Buf("ones")
    P.op("vector", lambda e: e.memset(ones, 1.0), [], [onesb])
    stg = [A.alloc(128, T) for _ in range(2)]
    stgb = [Buf("stg0"), Buf("stg1")]
    qkv = [A.alloc(128, T, BF16) for _ in range(3)]
    qkvb = [Buf("q"), Buf("k"), Buf("v")]
    Vd = [A.alloc(128, T, BF16) for _ in range(3)]
    Vdb = [Buf(f"Vd{i}") for i in range(3)]
    ACC = A.alloc(128, 2 * T).rearrange("p (a t) -> p a t", a=2)
    accb = Buf("acc")
    BT = A.alloc(128, 3 * 256).rearrange("p (b c) -> p b c", b=3)
    btb = Buf("bt")
    tmp = [A.alloc(128, 256) for _ in range(2)]
    tmpb = [Buf("tmp0"), Buf("tmp1")]
    pT = [A.alloc(128, 256, BF16) for _ in range(2)]
    pTb = [Buf("pT0"), Buf("pT1")]
    NB = T // 128
    bank = 0
    nld = 0
    for h in range(NH):
        P.dma(BT, bt_d[h].rearrange("b p c -> p b c"), writes=[btb])
        for i in range(3):
            s = nld % 2
            nld += 1
            P.dma(stg[s], qkvT[h, i], writes=[stgb[s]])
            if i == 0:
                P.op("scalar", lambda e, s=s: e.activation(out=qkv[0], in_=stg[s], func=AF.Copy, scale=128 ** -0.5),
                     [stgb[s]], [qkvb[0]])
            elif i == 1:
                P.op("vector", lambda e, s=s: e.tensor_copy(out=qkv[1], in_=stg[s]), [stgb[s]], [qkvb[1]])
            else:
                P.op("gpsimd", lambda e, s=s: e.tensor_copy(out=qkv[2], in_=stg[s]), [stgb[s]], [qkvb[2]])
        qs, kb, vb = qkv

        def cols(d, r, n):
            return slice(r + n * 128 * d, r + n * 128 * d + 127 * d + 1, d)

        for bi, d in enumerate(DILS):
            blocks = [(r, n) for r in range(d) for n in range(NB // d)]
            for g0 in range(0, NB, 8):
                bk = bank % 8
                bank += 1
                pt = C.banks[bk][:].bitcast(BF16)
                for u in range(8):
                    r, n = blocks[g0 + u]
                    P.op("tensor", lambda e, pt=pt, u=u, c=cols(d, r, n): e.transpose(pt[:, u * 128:(u + 1) * 128], vb[:, c], C.identb),
                         [qkvb[2], C.ident_buf], [C.bank_bufs[bk]])
                P.op("scalar", lambda e, pt=pt, bi=bi, g0=g0: e.copy(out=Vd[bi][:, g0 * 128:(g0 + 8) * 128], in_=pt),
                     [C.bank_bufs[bk]], [Vdb[bi]])
        it = 0
        for bi, d in enumerate(DILS):
            nper = NB // d
            for r in range(d):
                for n in range(nper):
                    blk = r * nper + n
                    bs = bank % 8
                    bo = (bank + 1) % 8
                    bank += 2
                    s = it % 2
                    it += 1
                    lo = 0 if n > 0 else 128
                    ps = C.banks[bs]
                    po = C.banks[bo]
                    qc = cols(d, r, n)
                    if n > 0:
                        P.op("tensor", lambda e, ps=ps, kc=cols(d, r, n - 1), qc=qc: e.matmul(ps[:, 0:128], lhsT=kb[:, kc], rhs=qs[:, qc], start=True, stop=True),
                             [qkvb[0], qkvb[1]], [C.bank_bufs[bs]])
                    P.op("tensor", lambda e, ps=ps, qc=qc: e.matmul(ps[:, 128:256], lhsT=kb[:, qc], rhs=qs[:, qc], start=True, stop=True),
                         [qkvb[0], qkvb[1]], [C.bank_bufs[bs]])
                    P.op("vector", lambda e, ps=ps, s=s, lo=lo, bi=bi: e.tensor_tensor(out=tmp[s][:, lo:256], in0=ps[:, lo:256], in1=BT[:, bi, lo:256], op=ALU.add),
                         [C.bank_bufs[bs], btb], [tmpb[s]])
                    P.op("scalar", lambda e, s=s, lo=lo: e.activation(out=pT[s][:, lo:256], in_=tmp[s][:, lo:256], func=AF.Exp),
                         [tmpb[s]], [pTb[s]])
                    if n > 0:
                        P.op("tensor", lambda e, po=po, s=s, bi=bi, blk=blk: e.matmul(po[:, 0:128], lhsT=Vd[bi][:, (blk - 1) * 128:blk * 128], rhs=pT[s][:, 0:128], start=True, stop=False),
                             [Vdb[bi], pTb[s]], [C.bank_bufs[bo]])
                    P.op("tensor", lambda e, po=po, s=s, bi=bi, blk=blk, n=n: e.matmul(po[:, 0:128], lhsT=Vd[bi][:, blk * 128:(blk + 1) * 128], rhs=pT[s][:, 128:256], start=(n == 0), stop=True),
                         [Vdb[bi], pTb[s]], [C.bank_bufs[bo]])
                    if n > 0:
                        P.op("tensor", lambda e, po=po, s=s: e.matmul(po[:, 128:256], lhsT=ones, rhs=pT[s][:, 0:128], start=True, stop=False),
                             [onesb, pTb[s]], [C.bank_bufs[bo]])
                    P.op("tensor", lambda e, po=po, s=s, n=n: e.matmul(po[:, 128:256], lhsT=ones, rhs=pT[s][:, 128:256], start=(n == 0), stop=True),
                         [onesb, pTb[s]], [C.bank_bufs[bo]])
                    pv_ = po[:, 0:256].rearrange("p (a q) -> p a q", a=2)
                    if bi == 0:
                        P.op("vector", lambda e, pv_=pv_, qc=qc: e.tensor_copy(out=ACC[:, :, qc], in_=pv_), [C.bank_bufs[bo]], [accb])
                    else:
                        P.op("vector", lambda e, pv_=pv_, qc=qc: e.tensor_tensor(out=ACC[:, :, qc], in0=ACC[:, :, qc], in1=pv_, op=ALU.add),
                             [C.bank_bufs[bo], accb], [accb])
        P.op("vector", lambda e: e.reciprocal(out=ACC[:, 1, :], in_=ACC[:, 1, :]), [accb], [accb])
        P.op("vector", lambda e: e.tensor_tensor(out=ACC[:, 0, :], in0=ACC[:, 0, :], in1=ACC[:, 1, :], op=ALU.mult), [accb], [accb])
        P.dma(oT[h], ACC[:, 0, :], reads=[accb])
    return C.finish()


DN_C = 64
DN_NEG = -30000.0


def build_dn(T=SEQ, NH=2):
    C = Ctx(52000)
    P, A = C.P, C.A
    NCK = T // DN_C
    NG = NCK // 8
    qkvp = C.dram_in("qkvp", [NH, 3, 128, T])
    convw = C.dram_in("convw", [NH, 128, 12])
    zt = C.dram_in("zt", [NH, T, 128])
    ab = C.dram_in("ab", [NH, 2, NCK, DN_C])
    hp = C.dram_in("hp", [NH, 128, 2])
    ng_d = C.dram_in("ng", [1, 128])
    y_out = C.dram_out("y", [NH, T, 128])
    scr = C.dram_tmp("scr", [NH, 5, T])
    scr_cd = C.dram_tmp("scr_cd", [NH, 1, NCK])
    C.make_ident()
    V = lambda fn, r, w: P.op("vector", fn, r, w)
    S_ = lambda fn, r, w: P.op("scalar", fn, r, w)
    G_ = lambda fn, r, w: P.op("gpsimd", fn, r, w)
    kb_ = Buf("const")
    MI = A.alloc(64, 64)
    MS2 = A.alloc(64, 64)
    STR = A.alloc(64, 64)
    I8 = A.alloc(64, 512)
    onesf = A.alloc(128, 128)
    ones64 = A.alloc(64, 64)
    epsc = A.alloc(128, 1)
    G_(lambda e: e.memset(MI, 0.0), [], [kb_])
    G_(lambda e: e.affine_select(out=MI, in_=MI, pattern=[[1, 64]], compare_op=ALU.is_ge, fill=DN_NEG, base=0, channel_multiplier=-1), [kb_], [kb_])
    G_(lambda e: e.memset(MS2, 0.0), [kb_], [kb_])
    G_(lambda e: e.affine_select(out=MS2, in_=MS2, pattern=[[-1, 64]], compare_op=ALU.is_gt, fill=DN_NEG, base=0, channel_multiplier=1), [kb_], [kb_])
    G_(lambda e: e.memset(STR, 1.0), [kb_], [kb_])
    G_(lambda e: e.affine_select(out=STR, in_=STR, pattern=[[1, 64]], compare_op=ALU.is_gt, fill=0.0, base=0, channel_multiplier=-1), [kb_], [kb_])
    G_(lambda e: e.memset(I8, 0.0), [kb_], [kb_])
    I8v = I8.rearrange("p (n c) -> p n c", c=64)
    G_(lambda e: e.affine_select(out=I8v, in_=I8v, pattern=[[0, 8], [-1, 64]], compare_op=ALU.not_equal, fill=1.0, base=0, channel_multiplier=1), [kb_], [kb_])
    V(lambda e: e.memset(onesf, 1.0), [], [kb_])
    V(lambda e: e.memset(ones64, 1.0), [kb_], [kb_])
    V(lambda e: e.memset(epsc, NORM_EPS), [kb_], [kb_])
    ngt = A.alloc(64, 128)
    P.dma(ngt, ng_d.partition_broadcast(64), writes=[kb_])
    qn, kn, vn = [A.alloc(128, T, BF16) for _ in range(3)]
    qnb, knb, vnb = Buf("qn"), Buf("kn"), Buf("vn")
    gt = A.alloc(64, 64 * 10).rearrange("p (i c) -> p i c", c=64)
    gtb = Buf("gates")
    hpt = A.alloc(128, 4)
    cwt = A.alloc(128, 12)
    cdr = A.alloc(128, NCK)
    cdrb = Buf("cdr")
    qegT = A.alloc(128, T, BF16)
    qegb = Buf("qeg")
    mTM = A.mark()
    TM = [A.alloc(64, NCK * 128, BF16).rearrange("p (n d) -> p n d", d=128) for _ in range(3)]
    TMb = [Buf("kbeg"), Buf("ktail"), Buf("vbeta")]
    ATT = A.alloc(64, T, BF16)
    attb = Buf("att")
    WTn = A.alloc(128, T, BF16)
    wtb = Buf("wt")
    X16 = A.alloc(64, T, BF16)
    x16b = Buf("x16")
    mE = A.mark()
    bank = [0]

    def nb():
        b = bank[0] % 8
        bank[0] += 1
        return b

    for h in range(NH):
        P.barrier()
        A.reset(mE)
        xin = A.alloc(128, T + 8)
        acc = A.alloc(128, T)
        sq = A.alloc(128, T)
        xb, ab_, sb_ = Buf("xin"), Buf("acc"), Buf("sq")
        P.dma(cwt, convw[h], writes=[gtb])
        P.dma(hpt[:, 0:2], hp[h], writes=[gtb])
        V(lambda e: e.memset(xin[:, 0:4], 0.0), [], [xb])
        for i, (dst, dstb) in enumerate(((qn, qnb), (kn, knb), (vn, vnb))):
            P.dma(xin[:, 3:3 + T], qkvp[h, i], writes=[xb])
            V(lambda e, i=i: e.tensor_scalar(out=acc, in0=xin[:, 0:T], scalar1=cwt[:, 4 * i:4 * i + 1], scalar2=None, op0=ALU.mult), [xb, gtb], [ab_])
            for j in (1, 2, 3):
                V(lambda e, i=i, j=j: e.scalar_tensor_tensor(out=acc, in0=xin[:, j:j + T], scalar=cwt[:, 4 * i + j:4 * i + j + 1], in1=acc,
                                                            op0=ALU.mult, op1=ALU.add), [xb, gtb, ab_], [ab_])
            S_(lambda e: e.activation(out=acc, in_=acc, func=AF.Silu), [ab_], [ab_])
            if i == 2:
                G_(lambda e, dst=dst: e.tensor_copy(out=dst, in_=acc), [ab_], [dstb])
                continue
            G_(lambda e: e.tensor_tensor(out=sq, in0=acc, in1=acc, op=ALU.mult), [ab_], [sb_])
            for q in range(T // 512):
                bk = nb()
                P.op("tensor", lambda e, bk=bk, q=q: e.matmul(C.banks[bk][:], lhsT=onesf, rhs=sq[:, q * 512:(q + 1) * 512], start=True, stop=True),
                     [kb_, sb_], [C.bank_bufs[bk]])
                rs = xin[:, 8 + q * 512:8 + (q + 1) * 512]
                S_(lambda e, bk=bk, rs=rs: e.activation(out=rs, in_=C.banks[bk][:], func=AF.Sqrt, bias=epsc, scale=1.0), [C.bank_bufs[bk], kb_], [xb])
                V(lambda e, rs=rs: e.reciprocal(out=rs, in_=rs), [xb], [xb])
                sc = 128 ** -0.5 if i == 0 else 1.0
                V(lambda e, rs=rs, q=q, dst=dst, sc=sc: e.scalar_tensor_tensor(out=dst[:, q * 512:(q + 1) * 512], in0=acc[:, q * 512:(q + 1) * 512], scalar=sc, in1=rs,
                                                                              op0=ALU.mult, op1=ALU.mult), [ab_, xb], [dstb])
        P.barrier()
        A.reset(mE)
        gcr = A.alloc(64, T)
        gcrb = Buf("gcr")
        STW = 1024
        stg = [A.alloc(128, STW) for _ in range(2)]
        stgb = [Buf("stg0"), Buf("stg1")]
        tbf = [A.alloc(128, T, BF16) for _ in range(2)]
        tbfb = [Buf("tbf0"), Buf("tbf1")]
        kbT = A.alloc(128, T, BF16)
        kbTb = Buf("kbT")
        DTr = A.alloc(64, T)
        dtrb = Buf("dtr")
        wk = [A.alloc(64, 512) for _ in range(10)]
        wkb = [Buf(f"wk{i}") for i in range(10)]
        a_t, b_t, g_t, gc_t, eg_t, et_t, beg_t, sp_t, gcT_t, cd_t = [gt[:, i, :] for i in range(10)]
        P.dma(a_t, ab[h, 0], writes=[gtb])
        P.dma(b_t, ab[h, 1], writes=[gtb])
        gs = lambda fn: S_(fn, [gtb], [gtb])
        gv = lambda fn: V(fn, [gtb, kb_], [gtb])
        gs(lambda e: e.activation(out=b_t, in_=b_t, func=AF.Sigmoid))
        gs(lambda e: e.activation(out=sp_t, in_=a_t, func=AF.Exp, bias=hpt[0:64, 1:2], scale=1.0))
        gs(lambda e: e.activation(out=sp_t, in_=sp_t, func=AF.Ln, bias=onesf[0:64, 0:1], scale=1.0))
        gs(lambda e: e.activation(out=hpt[:, 2:3], in_=hpt[:, 0:1], func=AF.Exp))
        gv(lambda e: e.tensor_scalar(out=g_t, in0=sp_t, scalar1=hpt[0:64, 2:3], scalar2=-1.0, op0=ALU.mult, op1=ALU.mult))
        gv(lambda e: e.tensor_tensor_scan(out=gc_t, data0=ones64, data1=g_t, initial=0.0, op0=ALU.mult, op1=ALU.add))
        gs(lambda e: e.activation(out=eg_t, in_=gc_t, func=AF.Exp))
        gs(lambda e: e.activation(out=et_t, in_=gc_t, func=AF.Exp, bias=gc_t[:, 63:64], scale=-1.0))
        gs(lambda e: e.activation(out=cd_t[:, 0:1], in_=gc_t[:, 63:64], func=AF.Exp))
        gv(lambda e: e.tensor_tensor(out=beg_t, in0=b_t, in1=eg_t, op=ALU.mult))
        scb = Buf("scr")
        for ri, src in enumerate((gc_t, b_t, eg_t, et_t, beg_t)):
            P.dma(scr[h, ri].rearrange("(n c) -> n c", c=DN_C), src, reads=[gtb], writes=[scb])
        P.dma(scr_cd[h, 0].rearrange("(n o) -> n o", o=1), cd_t[:, 0:1], reads=[gtb], writes=[scb])
        P.dma(gcr, scr[h, 0:1, :].partition_broadcast(64), reads=[scb], writes=[gcrb])
        P.dma(cdr, scr_cd[h, 0:1, :].partition_broadcast(128), reads=[scb], writes=[cdrb])
        bk = nb()
        P.op("tensor", lambda e, bk=bk: e.transpose(C.banks[bk][0:64, 0:64], gc_t, C.identf[0:64, 0:64]), [gtb, C.ident_buf], [C.bank_bufs[bk]])
        S_(lambda e, bk=bk: e.copy(out=gcT_t, in_=C.banks[bk][0:64, 0:64]), [C.bank_bufs[bk]], [gtb])
        nst = [0]

        def scaled(row, src, srcb, dst, dstb):
            for q in range(T // STW):
                s = nst[0] % 2
                nst[0] += 1
                P.dma(stg[s], scr[h, row:row + 1, q * STW:(q + 1) * STW].partition_broadcast(128), reads=[scb], writes=[stgb[s]])
                eng = "vector" if q % 2 == 0 else "gpsimd"
                P.op(eng, lambda e, s=s, q=q: e.tensor_tensor(out=dst[:, q * STW:(q + 1) * STW], in0=src[:, q * STW:(q + 1) * STW], in1=stg[s], op=ALU.mult),
                     [srcb, stgb[s]], [dstb])

        def to_tm(src, srcb, ti):
            for g in range(NG):
                bk = nb()
                pt = C.banks[bk][0:64, :].bitcast(BF16)
                for u in range(8):
                    n = g * 8 + u
                    P.op("tensor", lambda e, pt=pt, u=u, n=n: e.transpose(pt[:, u * 128:(u + 1) * 128], src[:, n * 64:(n + 1) * 64], C.identb),
                         [srcb, C.ident_buf], [C.bank_bufs[bk]])
                dst = TM[ti][:, g * 8:(g + 1) * 8, :]
                S_(lambda e, dst=dst, pt=pt: e.copy(out=dst, in_=pt.rearrange("p (n d) -> p n d", d=128)), [C.bank_bufs[bk]], [TMb[ti]])

        scaled(1, kn, knb, kbT, kbTb)
        scaled(2, qn, qnb, qegT, qegb)
        scaled(4, kn, knb, tbf[0], tbfb[0])
        to_tm(tbf[0], tbfb[0], 0)
        scaled(3, kn, knb, tbf[1], tbfb[1])
        to_tm(tbf[1], tbfb[1], 1)
        scaled(1, vn, vnb, tbf[0], tbfb[0])
        to_tm(tbf[0], tbfb[0], 2)
        gcr3 = gcr.rearrange("p (n c) -> p n c", c=64)
        DT3 = DTr.rearrange("p (n c) -> p n c", c=64)
        V(lambda e: e.tensor_tensor(out=DT3, in0=gcr3, in1=gcT_t.unsqueeze(2).to_broadcast([64, NCK, 64]), op=ALU.subtract), [gcrb, gtb], [dtrb])
        for g in range(NG):
            cs = slice(g * 512, (g + 1) * 512)
            dtg = DTr[:, cs].rearrange("p (n c) -> p n c", c=64)
            dec, dec2, decs, Nn0, Nt0, X0, Nn1, Nt1, X1, tq = wk
            decb, dec2b, decsb, Nn0b, Nt0b, X0b, Nn1b, Nt1b, X1b, tqb = wkb
            v3 = lambda ap: ap.rearrange("p (n c) -> p n c", c=64)
            MIb = MI.unsqueeze(1).to_broadcast([64, 8, 64])
            MS2b = MS2.unsqueeze(1).to_broadcast([64, 8, 64])
            STRb = STR.unsqueeze(1).to_broadcast([64, 8, 64])
            V(lambda e, dtg=dtg, MIb=MIb: e.tensor_tensor(out=v3(dec), in0=dtg, in1=MIb, op=ALU.add), [dtrb, kb_], [decb])
            S_(lambda e: e.activation(out=dec, in_=dec, func=AF.Exp), [decb], [decb])
            G_(lambda e, dtg=dtg, MS2b=MS2b: e.tensor_tensor(out=v3(dec2), in0=MS2b, in1=dtg, op=ALU.subtract), [dtrb, kb_], [dec2b])
            S_(lambda e: e.activation(out=dec2, in_=dec2, func=AF.Exp), [dec2b], [dec2b])
            G_(lambda e, STRb=STRb: e.tensor_tensor(out=v3(decs), in0=v3(dec), in1=STRb, op=ALU.mult), [decb, kb_], [decsb])
            ba, bb_, bc = nb(), nb(), nb()
            for u in range(8):
                n = g * 8 + u
                nc_ = slice(n * 64, (n + 1) * 64)
                us = slice(u * 64, (u + 1) * 64)
                P.op("tensor", lambda e, ba=ba, nc_=nc_, us=us: e.matmul(C.banks[ba][0:64, us], lhsT=kn[:, nc_], rhs=kbT[:, nc_], start=True, stop=True),
                     [knb, kbTb], [C.bank_bufs[ba]])
                P.op("tensor", lambda e, bb_=bb_, nc_=nc_, us=us: e.matmul(C.banks[bb_][0:64, us], lhsT=kbT[:, nc_], rhs=kn[:, nc_], start=True, stop=True),
                     [knb, kbTb], [C.bank_bufs[bb_]])
                P.op("tensor", lambda e, bc=bc, nc_=nc_, us=us: e.matmul(C.banks[bc][0:64, us], lhsT=kn[:, nc_], rhs=qn[:, nc_], start=True, stop=True),
                     [knb, qnb], [C.bank_bufs[bc]])
            V(lambda e, ba=ba: e.scalar_tensor_tensor(out=Nn0, in0=C.banks[ba][0:64, :], scalar=-1.0, in1=decs, op0=ALU.mult, op1=ALU.mult),
              [C.bank_bufs[ba], decsb], [Nn0b])
            V(lambda e, bb_=bb_: e.scalar_tensor_tensor(out=Nt0, in0=C.banks[bb_][0:64, :], scalar=-1.0, in1=dec2, op0=ALU.mult, op1=ALU.mult),
              [C.bank_bufs[bb_], dec2b], [Nt0b])
            V(lambda e, bc=bc, cs=cs: e.tensor_tensor(out=ATT[:, cs], in0=C.banks[bc][0:64, :], in1=dec, op=ALU.mult), [C.bank_bufs[bc], decb], [attb])
            G_(lambda e: e.tensor_tensor(out=X0, in0=Nn0, in1=I8, op=ALU.add), [Nn0b, kb_], [X0b])
            cur = (Nn0, Nn0b, Nt0, Nt0b, X0, X0b)
            oth = (Nn1, Nn1b, Nt1, Nt1b, X1, X1b)
            for lvl in range(1, 6):
                Nn, Nnb, Nt, Ntb, X, Xb = cur
                Nn_, Nn_b, Nt_, Nt_b, X_, X_b = oth
                bt_, bn_, bx_ = nb(), nb(), nb()
                for u in range(8):
                    us = slice(u * 64, (u + 1) * 64)
                    P.op("tensor", lambda e, bt_=bt_, us=us, Nn=Nn, Nt=Nt: e.matmul(C.banks[bt_][0:64, us], lhsT=Nn[:, us], rhs=Nt[:, us], start=True, stop=True),
                         [Nnb, Ntb], [C.bank_bufs[bt_]])
                    if lvl < 5:
                        P.op("tensor", lambda e, bn_=bn_, us=us, Nn=Nn, Nt=Nt: e.matmul(C.banks[bn_][0:64, us], lhsT=Nt[:, us], rhs=Nn[:, us], start=True, stop=True),
                             [Nnb, Ntb], [C.bank_bufs[bn_]])
                S_(lambda e, bt_=bt_, Nt_=Nt_: e.copy(out=Nt_, in_=C.banks[bt_][0:64, :]), [C.bank_bufs[bt_]], [Nt_b])
                if lvl < 5:
                    S_(lambda e, bn_=bn_, Nn_=Nn_: e.copy(out=Nn_, in_=C.banks[bn_][0:64, :]), [C.bank_bufs[bn_]], [Nn_b])
                for u in range(8):
                    us = slice(u * 64, (u + 1) * 64)
                    P.op("tensor", lambda e, bx_=bx_, us=us, Nt_=Nt_, X=X: e.matmul(C.banks[bx_][0:64, us], lhsT=Nt_[:, us], rhs=X[:, us], start=True, stop=True),
                         [Nt_b, Xb], [C.bank_bufs[bx_]])
                V(lambda e, bx_=bx_, X=X, X_=X_: e.tensor_tensor(out=X_, in0=C.banks[bx_][0:64, :], in1=X, op=ALU.add), [C.bank_bufs[bx_], Xb], [X_b])
                cur, oth = (Nn_, Nn_b, Nt_, Nt_b, X_, X_b), (Nn, Nnb, Nt, Ntb, X, Xb)
            Xf, Xfb = cur[4], cur[5]
            G_(lambda e, Xf=Xf, cs=cs: e.tensor_copy(out=X16[:, cs], in_=Xf), [Xfb], [x16b])
            bw = nb()
            for u in range(8):
                n = g * 8 + u
                us = slice(u * 64, (u + 1) * 64)
                P.op("tensor", lambda e, bw=bw, us=us, n=n: e.matmul(C.banks[bw][:, us], lhsT=TM[0][:, n, :], rhs=X16[:, n * 64:(n + 1) * 64], start=True, stop=True),
                     [TMb[0], x16b], [C.bank_bufs[bw]])
            S_(lambda e, bw=bw, cs=cs: e.activation(out=WTn[:, cs], in_=C.banks[bw][:], func=AF.Copy, scale=-1.0), [C.bank_bufs[bw]], [wtb])
        P.barrier()
        A.reset(mE)
        O = A.alloc(64, NCK * 128).rearrange("p (n d) -> p n d", d=128)
        ob_ = Buf("O")
        S32 = A.alloc(128, 128)
        S16 = A.alloc(128, 128, BF16)
        s32b, s16b = Buf("S32"), Buf("S16")
        vnw = [A.alloc(64, 128, BF16) for _ in range(2)]
        vnwb = [Buf("vn0"), Buf("vn1")]
        m5 = A.mark()
        V(lambda e: e.memset(S32, 0.0), [], [s32b])
        V(lambda e: e.memset(S16, 0.0), [], [s16b])
        for n in range(NCK):
            nc_ = slice(n * 64, (n + 1) * 64)
            k2 = n % 2
            bv, bo, bs = nb(), nb(), nb()
            pv_, po_, ps_ = C.banks[bv][0:64, 0:128], C.banks[bo][0:64, 0:128], C.banks[bs][:, 0:128]
            P.op("tensor", lambda e, pv_=pv_, nc_=nc_, n=n: e.matmul(pv_, lhsT=X16[:, nc_], rhs=TM[2][:, n, :], start=True, stop=False),
                 [x16b, TMb[2]], [C.bank_bufs[bv]])
            P.op("tensor", lambda e, pv_=pv_, nc_=nc_: e.matmul(pv_, lhsT=WTn[:, nc_], rhs=S16, start=False, stop=True),
                 [wtb, s16b], [C.bank_bufs[bv]])
            S_(lambda e, pv_=pv_, k2=k2: e.copy(out=vnw[k2], in_=pv_), [C.bank_bufs[bv]], [vnwb[k2]])
            P.op("tensor", lambda e, po_=po_, nc_=nc_: e.matmul(po_, lhsT=qegT[:, nc_], rhs=S16, start=True, stop=False),
                 [qegb, s16b], [C.bank_bufs[bo]])
            P.op("tensor", lambda e, po_=po_, nc_=nc_, k2=k2: e.matmul(po_, lhsT=ATT[:, nc_], rhs=vnw[k2], start=False, stop=True),
                 [attb, vnwb[k2]], [C.bank_bufs[bo]])
            P.op("tensor", lambda e, ps_=ps_, n=n, k2=k2: e.matmul(ps_, lhsT=TM[1][:, n, :], rhs=vnw[k2], start=True, stop=True),
                 [TMb[1], vnwb[k2]], [C.bank_bufs[bs]])
            V(lambda e, ps_=ps_, n=n: e.scalar_tensor_tensor(out=S32, in0=S32, scalar=cdr[:, n:n + 1], in1=ps_, op0=ALU.mult, op1=ALU.add),
              [s32b, cdrb, C.bank_bufs[bs]], [s32b])
            S_(lambda e: e.copy(out=S16, in_=S32), [s32b], [s16b])
            V(lambda e, po_=po_, n=n: e.tensor_copy(out=O[:, n, :], in_=po_), [C.bank_bufs[bo]], [ob_])
        P.barrier()
        A.reset(mTM)
        Z = A.alloc(64, NCK * 128).rearrange("p (n d) -> p n d", d=128)
        zb = Buf("Z")
        A.reset(m5)
        ssq = A.alloc(64, 2 * NCK)
        ssb = Buf("ss")
        P.dma(Z, zt[h].rearrange("(n c) d -> c n d", c=DN_C), writes=[zb])
        S_(lambda e: e.activation(out=Z, in_=Z, func=AF.Silu), [zb], [zb])
        O2 = A.alloc(64, NCK * 128).rearrange("p (n d) -> p n d", d=128)
        o2b = Buf("O2")
        G_(lambda e: e.tensor_tensor(out=O2, in0=O, in1=O, op=ALU.mult), [ob_], [o2b])
        V(lambda e: e.tensor_reduce(out=ssq[:, 0:NCK], in_=O2, axis=AX.X, op=ALU.add), [o2b], [ssb])
        V(lambda e: e.tensor_scalar(out=ssq[:, 0:NCK], in0=ssq[:, 0:NCK], scalar1=1.0 / 128, scalar2=NORM_EPS, op0=ALU.mult, op1=ALU.add), [ssb], [ssb])
        S_(lambda e: e.sqrt(out=ssq[:, 0:NCK], in_=ssq[:, 0:NCK]), [ssb], [ssb])
        V(lambda e: e.reciprocal(out=ssq[:, 0:NCK], in_=ssq[:, 0:NCK]), [ssb], [ssb])
        V(lambda e: e.tensor_tensor(out=O, in0=O, in1=ssq[:, 0:NCK].unsqueeze(2).to_broadcast([64, NCK, 128]), op=ALU.mult), [ob_, ssb], [ob_])
        G_(lambda e: e.tensor_tensor(out=O, in0=O, in1=ngt.unsqueeze(1).to_broadcast([64, NCK, 128]), op=ALU.mult), [ob_, kb_], [ob_])
        V(lambda e: e.tensor_tensor(out=O, in0=O, in1=Z, op=ALU.mult), [ob_, zb], [ob_])
        P.dma(y_out[h].rearrange("(n c) d -> c n d", c=DN_C), O, reads=[ob_])
    return C.finish()


def dn_inputs(projT_b, conv_w, a_log, dt_bias, norm_gain, heads):
    T = projT_b.shape[1]
    NHh = len(heads)
    base_qkv = 512 + 3 * 768
    base_z = base_qkv + 3 * 768
    base_a = base_z + 768
    base_b = base_a + 6
    qkvp = np.empty((NHh, 3, 128, T), np.float32)
    convw = np.empty((NHh, 128, 12), np.float32)
    zt = np.empty((NHh, T, 128), np.float32)
    ab = np.empty((NHh, 2, T // DN_C, DN_C), np.float32)
    hp = np.empty((NHh, 128, 2), np.float32)
    for s_, h in enumerate(heads):
        for i in range(3):
            c0 = i * 768 + h * 128
            qkvp[s_, i] = projT_b[base_qkv + c0:base_qkv + c0 + 128]
            convw[s_, :, 4 * i:4 * i + 4] = conv_w[:, c0:c0 + 128].T
        zt[s_] = projT_b[base_z + h * 128:base_z + (h + 1) * 128].T
        ab[s_, 0] = projT_b[base_a + h].reshape(T // DN_C, DN_C)
        ab[s_, 1] = projT_b[base_b + h].reshape(T // DN_C, DN_C)
        hp[s_, :, 0] = a_log[h]
        hp[s_, :, 1] = dt_bias[h]
    return dict(qkvp=qkvp, convw=convw, zt=zt, ab=ab, hp=hp, ng=np.ascontiguousarray(norm_gain.reshape(1, 128), dtype=np.float32))


GELU_C = 2.0 * math.sqrt(2.0 / math.pi)


def build_post(nt=NT):
    C = Ctx(52000)
    P, A = C.P, C.A
    tok = nt * 128
    x = C.dram_in("x", [tok, D_MODEL])
    ys = C.dram_in("ys", [tok, 512])
    ydn = C.dram_in("ydn", [tok, 768])
    oat = C.dram_in("oat", [tok, 768])
    glu_w = C.dram_in("glu_w", [512, 512])
    vecs = C.dram_in("vecs", [1, 512 + 512 + 768 + D_MODEL])
    wout = C.dram_in("wout", [D_MODEL, D_MODEL])
    xo = C.dram_out("xo", [tok, D_MODEL])
    C.make_ident()
    V = lambda fn, r, w: P.op("vector", fn, r, w)
    S_ = lambda fn, r, w: P.op("scalar", fn, r, w)
    G_ = lambda fn, r, w: P.op("gpsimd", fn, r, w)
    vt, vtb = load_bcast(C, vecs[0:1, :], 512 + 512 + 768 + D_MODEL, "vecs")
    glub, sgain, again, gain3 = vt[:, 0:512], vt[:, 512:1024], vt[:, 1024:1792], vt[:, 1792:1792 + D_MODEL]
    wo = A.alloc(128, NDC * D_MODEL, BF16).rearrange("p (c d) -> p c d", d=D_MODEL)
    wob = [Buf(f"wo{q}") for q in range(4)]
    wout_v = wout.rearrange("(c p) d -> p c d", p=128)
    for q in range(4):
        P.dma(wo[:, :, q * 512:(q + 1) * 512], wout_v[:, :, q * 512:(q + 1) * 512], writes=[wob[q]], eng="gpsimd")
    gw = A.alloc(128, 4 * 512, BF16).rearrange("p (c d) -> p c d", d=512)
    gwb = Buf("gw")
    P.dma(gw, glu_w.rearrange("(c p) d -> p c d", p=128), writes=[gwb], eng="gpsimd")
    yt = A.alloc(128, 512); ytb = Buf("yt")
    t1 = A.alloc(128, 512); t1b = Buf("t1")
    gg = A.alloc(128, 512); ggb = Buf("gg")
    gb16 = A.alloc(128, 512, BF16); gb16b = Buf("gb16")
    gT = A.alloc(128, 512, BF16).rearrange("p (c t) -> p c t", t=128); gTb = Buf("gT")
    dnt = A.alloc(128, 768); dntb = Buf("dnt")
    att = A.alloc(128, 768); attb = Buf("att")
    mixn = A.alloc(128, D_MODEL, BF16); mixb = Buf("mixn")
    mixT = A.alloc(128, D_MODEL, BF16).rearrange("p (c t) -> p c t", t=128); mixTb = Buf("mixT")
    yo = A.alloc(128, D_MODEL); yob = Buf("yo")
    xt = A.alloc(128, D_MODEL); xtb = Buf("xt")
    junk = A.alloc(128, D_MODEL, BF16)
    st = A.alloc(128, 8); stb = [Buf("st0"), Buf("st1"), Buf("st2")]
    bank = [0]

    def nb():
        b_ = bank[0] % 8
        bank[0] += 1
        return b_

    for t in range(nt):
        rows = slice(t * 128, (t + 1) * 128)
        P.dma(yt, ys[rows, :], writes=[ytb])
        P.dma(dnt, ydn[rows, :], writes=[dntb])
        P.dma(att, oat[rows, :], writes=[attb])
        P.dma(xt, x[rows, :], writes=[xtb])
        V(lambda e: e.tensor_tensor(out=t1, in0=yt, in1=yt, op=ALU.mult), [ytb], [t1b])
        V(lambda e: e.tensor_scalar(out=t1, in0=t1, scalar1=0.044715, scalar2=1.0, op0=ALU.mult, op1=ALU.add), [t1b], [t1b])
        V(lambda e: e.tensor_tensor(out=t1, in0=t1, in1=yt, op=ALU.mult), [t1b, ytb], [t1b])
        S_(lambda e: e.activation(out=t1, in_=t1, func=AF.Sigmoid, scale=GELU_C), [t1b], [t1b])
        V(lambda e: e.tensor_tensor(out=gg, in0=t1, in1=yt, op=ALU.mult), [t1b, ytb], [ggb])
        G_(lambda e: e.tensor_copy(out=gb16, in_=gg), [ggb], [gb16b])
        bk = nb()
        pt = C.banks[bk][:].bitcast(BF16)
        for c in range(4):
            P.op("tensor", lambda e, pt=pt, c=c: e.transpose(pt[:, c * 128:(c + 1) * 128], gb16[:, c * 128:(c + 1) * 128], C.identb),
                 [gb16b, C.ident_buf], [C.bank_bufs[bk]])
        S_(lambda e, pt=pt: e.copy(out=gT, in_=pt[:, 0:512].rearrange("p (c t) -> p c t", t=128)), [C.bank_bufs[bk]], [gTb])
        bk = nb()
        for c in range(4):
            P.op("tensor", lambda e, bk=bk, c=c: e.matmul(C.banks[bk][:], lhsT=gT[:, c, :], rhs=gw[:, c, :], start=(c == 0), stop=(c == 3)),
                 [gTb, gwb], [C.bank_bufs[bk]])
        V(lambda e, bk=bk: e.tensor_tensor(out=t1, in0=C.banks[bk][:], in1=glub, op=ALU.add), [C.bank_bufs[bk], vtb], [t1b])
        S_(lambda e: e.activation(out=t1, in_=t1, func=AF.Sigmoid), [t1b], [t1b])
        V(lambda e: e.tensor_tensor(out=gg, in0=gg, in1=t1, op=ALU.mult), [ggb, t1b], [ggb])
        rms_rstd(C, gg, st[:, 0:1], st[:, 1:2], junk[:, 0:512], 512, ggb, stb[0])
        V(lambda e: e.scalar_tensor_tensor(out=mixn[:, 0:512], in0=gg, scalar=st[:, 1:2], in1=sgain, op0=ALU.mult, op1=ALU.mult),
          [ggb, stb[0], vtb], [mixb])
        G_(lambda e: e.tensor_copy(out=mixn[:, 512:1280], in_=dnt), [dntb], [mixb])
        rms_rstd(C, att, st[:, 2:3], st[:, 3:4], junk[:, 0:768], 768, attb, stb[1])
        V(lambda e: e.scalar_tensor_tensor(out=mixn[:, 1280:2048], in0=att, scalar=st[:, 3:4], in1=again, op0=ALU.mult, op1=ALU.mult),
          [attb, stb[1], vtb], [mixb])
        for half in range(2):
            bk = nb()
            pt = C.banks[bk][:].bitcast(BF16)
            for c8 in range(8):
                c = half * 8 + c8
                P.op("tensor", lambda e, pt=pt, c=c, c8=c8: e.transpose(pt[:, c8 * 128:(c8 + 1) * 128], mixn[:, c * 128:(c + 1) * 128], C.identb),
                     [mixb, C.ident_buf], [C.bank_bufs[bk]])
            S_(lambda e, pt=pt, half=half: e.copy(out=mixT[:, half * 8:(half + 1) * 8, :], in_=pt.rearrange("p (c t) -> p c t", t=128)),
               [C.bank_bufs[bk]], [mixTb])
        for q in range(4):
            bk = nb()
            for c in range(NDC):
                P.op("tensor", lambda e, bk=bk, c=c, q=q: e.matmul(C.banks[bk][:], lhsT=mixT[:, c, :], rhs=wo[:, c, q * 512:(q + 1) * 512],
                                                                  start=(c == 0), stop=(c == NDC - 1)), [mixTb, wob[q]], [C.bank_bufs[bk]])
            if q % 2 == 0:
                S_(lambda e, bk=bk, q=q: e.copy(out=yo[:, q * 512:(q + 1) * 512], in_=C.banks[bk][:]), [C.bank_bufs[bk]], [yob])
            else:
                V(lambda e, bk=bk, q=q: e.tensor_copy(out=yo[:, q * 512:(q + 1) * 512], in_=C.banks[bk][:]), [C.bank_bufs[bk]], [yob])
        rms_rstd(C, yo, st[:, 4:5], st[:, 5:6], junk, D_MODEL, yob, stb[2])
        V(lambda e: e.scalar_tensor_tensor(out=yo, in0=yo, scalar=st[:, 5:6], in1=gain3, op0=ALU.mult, op1=ALU.mult), [yob, stb[2], vtb], [yob])
        G_(lambda e: e.tensor_tensor(out=yo, in0=yo, in1=xt, op=ALU.add), [yob, xtb], [yob])
        P.dma(xo[rows, :], yo, reads=[yob])
    return C.finish()


_PROGS = {}


def _prog(name, builder):
    if name not in _PROGS:
        _PROGS[name] = builder()
    return _PROGS[name]


def _run(name, builder, in_maps):
    nc = _prog(name, builder)
    res = run_bass_kernel_spmd(nc, in_maps, core_ids=list(range(NCORES)))
    return res.results


def _c(a):
    return np.ascontiguousarray(a, dtype=np.float32)


OFF_AQ, OFF_AK, OFF_AV = 512, 1280, 2048


def kernel(**inp):
    x = _c(inp["x"]).reshape(BATCH * SEQ, D_MODEL)
    gains = inp["norm_gains"]
    bt_all = attn_bias_tables(np.asarray(inp["rel_bias"], np.float32))
    for l in range(DEPTH):
        def ffn(x, i):
            g = _c(np.stack([gains[l, 0 if i == 0 else 4], gains[l, 1 if i == 0 else 5]]))
            wg, wu, wd = _c(inp["ffn_w_gate"][l, i]), _c(inp["ffn_w_up"][l, i]), _c(inp["ffn_w_down"][l, i])
            r = _run("ffn", build_ffn, [dict(x=_c(x[c * TOK:(c + 1) * TOK]), gains=g, wg=wg, wu=wu, wd=wd) for c in range(NCORES)])
            return np.concatenate([r[c]["xo"] for c in range(NCORES)], 0)

        x = ffn(x, 0)
        win = np.zeros((D_MODEL, NPC * 128), np.float32)
        win[:, :N_IN_COLS] = inp["w_in"][l]
        g2 = _c(gains[l, 2].reshape(1, D_MODEL))
        r = _run("proj", build_proj, [dict(x=_c(x[c * TOK:(c + 1) * TOK]), gains=g2, win=win) for c in range(NCORES)])
        projT = np.concatenate([r[c]["projT"] for c in range(NCORES)], 1)
        s5_maps, at_maps, dn_maps = [], [], []
        for c in range(NCORES):
            b, j = c // 4, c % 4
            pb = projT[:, b * SEQ:(b + 1) * SEQ]
            gs = slice(8 * j, 8 * j + 8)
            s5_maps.append(s5_inputs(pb[128 * j:128 * (j + 1)], inp["ssm_lambda_re"][l, gs], inp["ssm_lambda_im"][l, gs],
                                     inp["ssm_log_dt"][l, gs], inp["ssm_b_re"][l, gs], inp["ssm_b_im"][l, gs],
                                     inp["ssm_c_re"][l, gs], inp["ssm_c_im"][l, gs], inp["ssm_d"][l, 128 * j:128 * (j + 1)]))
            heads = [j, 4 + j if 4 + j < 6 else j]
            qkvT = np.stack([np.stack([pb[off + h * 128:off + (h + 1) * 128] for off in (OFF_AQ, OFF_AK, OFF_AV)]) for h in heads])
            at_maps.append(dict(qkvT=_c(qkvT), bt=_c(bt_all[heads].reshape(2, 3, 128, 256))))
            dn_maps.append(dn_inputs(pb, inp["dn_conv_w"][l], inp["dn_a_log"][l], inp["dn_dt_bias"][l], inp["dn_norm_gain"][l], heads))
        rs5 = _run("s5", build_s5, s5_maps)
        rat = _run("attn", build_attn, at_maps)
        rdn = _run("dn", build_dn, dn_maps)
        ys = np.empty((BATCH * SEQ, 512), np.float32)
        ydn = np.empty((BATCH * SEQ, 768), np.float32)
        oat = np.empty((BATCH * SEQ, 768), np.float32)
        for c in range(NCORES):
            b, j = c // 4, c % 4
            rows = slice(b * SEQ, (b + 1) * SEQ)
            ys[rows, 128 * j:128 * (j + 1)] = rs5[c]["yT"].T
            for s_, h in enumerate([j, 4 + j]):
                if h < 6:
                    oat[rows, h * 128:(h + 1) * 128] = rat[c]["oT"][s_].T
                    ydn[rows, h * 128:(h + 1) * 128] = rdn[c]["y"][s_]
        vecs = _c(np.concatenate([inp["ssm_glu_b"][l], inp["ssm_out_gain"][l], inp["attn_out_gain"][l], gains[l, 3]]).reshape(1, -1))
        glu_w, wout = _c(inp["ssm_glu_w"][l]), _c(inp["w_out"][l])
        r = _run("post", build_post, [dict(x=_c(x[c * TOK:(c + 1) * TOK]), ys=_c(ys[c * TOK:(c + 1) * TOK]), ydn=_c(ydn[c * TOK:(c + 1) * TOK]),
                                           oat=_c(oat[c * TOK:(c + 1) * TOK]), glu_w=glu_w, vecs=vecs, wout=wout) for c in range(NCORES)])
        x = np.concatenate([r[c]["xo"] for c in range(NCORES)], 0)
        x = ffn(x, 1)
    return x.reshape(BATCH, SEQ, D_MODEL).astype(np.float32)
```
